# Optimizing a Trainium2 kernel written in Bass

```python
import jax, jax.numpy as jnp
from jax import lax
import numpy as np

D_MODEL = 1024
BATCH = 16
SEQ = 2048
DEPTH = 1

D_FF = 2816
W_A = D_MODEL // 2
H_A = 8
HD_A = W_A // H_A
W_B = D_MODEL - W_A
G_B = 8
CHUNK = 128
CONV_K = 31
N_MOD = 9
EPS = 1e-6
HALF = 0.5

kernel_name = "hybrid_gmlp_conformer_macaron_adaln"


def _rms_norm(x, g):
    xf = x.astype(jnp.float32)
    xf = xf * lax.rsqrt(jnp.mean(xf * xf, axis=-1, keepdims=True) + EPS)
    return (xf * g.astype(jnp.float32)).astype(x.dtype)


def _layer_norm(x, g, b):
    xf = x.astype(jnp.float32)
    mu = jnp.mean(xf, axis=-1, keepdims=True)
    xc = xf - mu
    var = jnp.mean(xc * xc, axis=-1, keepdims=True)
    y = xc * lax.rsqrt(var + EPS) * g.astype(jnp.float32) + b.astype(jnp.float32)
    return y.astype(x.dtype)


def _modulate(h, shift, scale):
    return h * (1 + scale[:, None, :]) + shift[:, None, :]


def _swiglu(h, w_in, w_out):
    gate, up = jnp.split(h @ w_in, 2, axis=-1)
    return (jax.nn.silu(gate) * up) @ w_out


def _hybrid_mixer(h, w_mix_in, gmlp_norm_g, gmlp_norm_b, w_spatial, b_spatial,
                  conv_w, conv_b, conv_norm_g, conv_norm_b, g_out_a, g_out_b, w_mix_out):
    bsz, seq, _ = h.shape
    proj = h @ w_mix_in
    u, v, a, g = jnp.split(proj, [W_A, 2 * W_A, 2 * W_A + W_B], axis=-1)

    v = _layer_norm(v, gmlp_norm_g, gmlp_norm_b)
    n_chunks = seq // CHUNK
    v = v.reshape(bsz, n_chunks, CHUNK, H_A, HD_A)
    causal = jnp.tril(jnp.ones((CHUNK, CHUNK), dtype=bool))
    w_s = jnp.where(causal[None], w_spatial, jnp.zeros_like(w_spatial))
    z = jnp.einsum('hts,bnshd->bnthd', w_s, v) + b_spatial.T[None, None, :, :, None]
    y_a = u * z.reshape(bsz, seq, W_A)

    glu = a * jax.nn.sigmoid(g)
    conv = lax.conv_general_dilated(
        glu, conv_w[:, None, :], window_strides=(1,), padding=[(CONV_K - 1, 0)],
        dimension_numbers=('NWC', 'WIO', 'NWC'), feature_group_count=W_B) + conv_b
    y_b = jax.nn.silu(_layer_norm(conv, conv_norm_g, conv_norm_b))

    y = jnp.concatenate([_rms_norm(y_a, g_out_a), _rms_norm(y_b, g_out_b)], axis=-1)
    return y @ w_mix_out


def setup_inputs(seed: int = 0) -> dict:
    key = jax.random.key(seed)
    ks = jax.random.split(key, 32)
    L, D = DEPTH, D_MODEL

    def nrm(k, shape, std):
        return std * jax.random.normal(k, shape, jnp.float32)

    def gain(k, shape):
        return 1.0 + 0.05 * jax.random.normal(k, shape, jnp.float32)

    return {
        "x": nrm(ks[0], (BATCH, SEQ, D), 1.0),
        "c": nrm(ks[1], (BATCH, D), 1.0),
        "w_ada": nrm(ks[2], (L, D, N_MOD * D), 0.5 * D ** -0.5),
        "b_ada": nrm(ks[3], (L, N_MOD * D), 0.02),
        "g_pre_f1": gain(ks[4], (L, D)),
        "g_post_f1": gain(ks[5], (L, D)),
        "w_f1_in": nrm(ks[6], (L, D, 2 * D_FF), D ** -0.5),
        "w_f1_out": nrm(ks[7], (L, D_FF, D), D_FF ** -0.5),
        "g_pre_m": gain(ks[8], (L, D)),
        "g_post_m": gain(ks[9], (L, D)),
        "w_mix_in": nrm(ks[10], (L, D, 2 * W_A + 2 * W_B), D ** -0.5),
        "gmlp_norm_g": gain(ks[11], (L, W_A)),
        "gmlp_norm_b": nrm(ks[12], (L, W_A), 0.02),
        "w_spatial": nrm(ks[13], (L, H_A, CHUNK, CHUNK), CHUNK ** -0.5),
        "b_spatial": gain(ks[14], (L, H_A, CHUNK)),
        "conv_w": nrm(ks[15], (L, CONV_K, W_B), CONV_K ** -0.5),
        "conv_b": nrm(ks[16], (L, W_B), 0.02),
        "conv_norm_g": gain(ks[17], (L, W_B)),
        "conv_norm_b": nrm(ks[18], (L, W_B), 0.02),
        "g_out_a": gain(ks[19], (L, W_A)),
        "g_out_b": gain(ks[20], (L, W_B)),
        "w_mix_out": nrm(ks[21], (L, W_A + W_B, D), (W_A + W_B) ** -0.5),
        "g_pre_f2": gain(ks[22], (L, D)),
        "g_post_f2": gain(ks[23], (L, D)),
        "w_f2_in": nrm(ks[24], (L, D, 2 * D_FF), D ** -0.5),
        "w_f2_out": nrm(ks[25], (L, D_FF, D), D_FF ** -0.5),
    }


def reference(x, c, w_ada, b_ada, g_pre_f1, g_post_f1, w_f1_in, w_f1_out,
              g_pre_m, g_post_m, w_mix_in, gmlp_norm_g, gmlp_norm_b, w_spatial, b_spatial,
              conv_w, conv_b, conv_norm_g, conv_norm_b, g_out_a, g_out_b, w_mix_out,
              g_pre_f2, g_post_f2, w_f2_in, w_f2_out):
    for l in range(DEPTH):
        ada = jax.nn.silu(c) @ w_ada[l] + b_ada[l]
        sh1, sc1, gt1, sh2, sc2, gt2, sh3, sc3, gt3 = jnp.split(ada, N_MOD, axis=-1)

        h = _modulate(_rms_norm(x, g_pre_f1[l]), sh1, sc1)
        x = x + HALF * gt1[:, None, :] * _rms_norm(_swiglu(h, w_f1_in[l], w_f1_out[l]), g_post_f1[l])

        h = _modulate(_rms_norm(x, g_pre_m[l]), sh2, sc2)
        y = _hybrid_mixer(h, w_mix_in[l], gmlp_norm_g[l], gmlp_norm_b[l], w_spatial[l], b_spatial[l],
                          conv_w[l], conv_b[l], conv_norm_g[l], conv_norm_b[l],
                          g_out_a[l], g_out_b[l], w_mix_out[l])
        x = x + gt2[:, None, :] * _rms_norm(y, g_post_m[l])

        h = _modulate(_rms_norm(x, g_pre_f2[l]), sh3, sc3)
        x = x + HALF * gt3[:, None, :] * _rms_norm(_swiglu(h, w_f2_in[l], w_f2_out[l]), g_post_f2[l])
    return x
```

```python
import contextlib
import numpy as np
import concourse.bass as bass
import concourse.mybir as mybir
from concourse.bass_utils import run_bass_kernel_spmd

F32, BF16, I32 = mybir.dt.float32, mybir.dt.bfloat16, mybir.dt.int32
AF = mybir.ActivationFunctionType
ALU = mybir.AluOpType

D = 1024
DFF = 2816
NCORES = 8
TOK = 4096
TB = 1024
NBLK = 4
EPS = 1e-6
R = 8
MAGIC = float(0x5F3759DF)


class Tok:
    __slots__ = ("sem", "val", "key")

    def __init__(self, sem, val, key):
        self.sem, self.val, self.key = sem, val, key


class Q:
    def __init__(self, nc, eng, st, name):
        self.e = eng
        self.sem = st.enter_context(nc.semaphore(name))
        self.key = name
        self.cnt = 0
        self.seen = {}

    def wait(self, *toks):
        for tok in toks:
            if tok is None:
                continue
            if isinstance(tok, (list, tuple)):
                self.wait(*tok)
                continue
            if self.seen.get(tok.key, 0) >= tok.val:
                continue
            self.e.wait_ge(tok.sem, tok.val)
            self.seen[tok.key] = tok.val

    def done(self, inst):
        self.cnt += 1
        inst.then_inc(self.sem, 1)
        return Tok(self.sem, self.cnt, self.key)


class DmaSem:
    def __init__(self, nc, st, name):
        self.sem = st.enter_context(nc.semaphore(name))
        self.key = name
        self.cnt = 0

    def add(self, inst):
        self.cnt += 16
        inst.then_inc(self.sem, 16)
        return Tok(self.sem, self.cnt, self.key)

    def tok(self):
        return Tok(self.sem, self.cnt, self.key)


def build(nsub=3, nblk=NBLK):
    nc = bass.Bass("TRN2", target_bir_lowering=False)

    def din(name, shape):
        return nc.dram_tensor(name, shape, F32, kind="ExternalInput").ap()

    x_in = din("x", [TOK, D])
    c_in = din("c", [2, D])
    w_ada = din("w_ada", [D, 9 * D])
    b_ada = din("b_ada", [9 * D])
    g_pre = [din("g_pre_f1", [D]), din("g_pre_m", [D]), din("g_pre_f2", [D])]
    g_post = [din("g_post_f1", [D]), din("g_post_m", [D]), din("g_post_f2", [D])]
    w_in = {0: din("w_f1_in", [D, 2 * DFF]), 2: din("w_f2_in", [D, 2 * DFF])}
    w_out = {0: din("w_f1_out", [DFF, D]), 2: din("w_f2_out", [DFF, D])}
    w_mix_in = din("w_mix_in", [D, 2048])
    w_mix_out = din("w_mix_out", [D, D])
    gmlp_g = din("gmlp_norm_g", [512])
    gmlp_b = din("gmlp_norm_b", [512])
    w_sp = din("w_spatial", [8, 128, 128])
    b_sp = din("b_spatial", [8, 128])
    conv_w = din("conv_w", [31, 512])
    conv_b = din("conv_b", [512])
    cn_g = din("conv_norm_g", [512])
    cn_b = din("conv_norm_b", [512])
    g_oa = din("g_out_a", [512])
    g_ob = din("g_out_b", [512])
    out = nc.dram_tensor("out", [TOK, D], F32, kind="ExternalOutput").ap()

    with contextlib.ExitStack() as st:
        E = st.enter_context

        def sb(name, shape, dt):
            return E(nc.sbuf_tensor(name, shape, dt))

        xres = sb("xres", [128, 8, 1024], F32)
        hT = sb("hT", [128, 8, 1024], BF16)
        SC = sb("SC", [128, 22528], BF16)
        slots = [sb(f"ring{i}", [128, 4096], BF16) for i in range(R)]
        GR = sb("GR", [128, 1024], F32)
        GBP = sb("GBP", [128, 2, 1024], F32)
        gg = sb("gg", [128, 1024], F32)
        GB = sb("GB", [128, 4, 512], F32)
        T32 = sb("T32", [128, 6, 512], F32)
        vn = sb("vn", [128, 2, 512], BF16)
        ynb = sb("ynb", [128, 2, 2, 512], BF16)
        ident = sb("ident", [128, 128], BF16)
        identf = sb("identf", [128, 128], F32)
        SEL = sb("SEL", [128, 2, 128], F32)
        WsT = sb("WsT", [128, 8, 128], BF16)
        IND = sb("IND", [8, 512], F32)
        bsp = sb("bsp", [8, 128], F32)
        COLS = sb("COLS", [128, 128], F32)
        convw = sb("convw", [128, 4, 31], F32)
        MOD = sb("MOD", [128, 3, 2, 8, 2], F32)
        ADA = sb("ADA", [128, 48, 2], F32)
        ST = sb("ST", [128, 256], F32)
        HALO = sb("HALO", [128, 4, 30], BF16)
        cTb = sb("cTb", [128, 8, 2], BF16)
        NH = sb("NH", [128, 8], F32)
        SSX = sb("SSX", [128, 16], F32)
        P = [E(nc.psum_tensor(f"P{i}", [128, 512], F32)) for i in range(8)]

        actT = SC[:, :].rearrange("p (f n) -> p f n", f=22)
        xn = SC[:, 0:4096].rearrange("p (t n) -> p t n", t=4)
        yT = SC[:, 0:8192].rearrange("p (k n) -> p k n", k=8)
        DG = SC[:, 4096:4096 + 3968].rearrange("p (k n) -> p k n", k=31)
        convS = SC[:, 8192:16384].bitcast(F32).rearrange("p (c n) -> p c n", c=4)
        glu = SC[:, 16384:16384 + 4216].rearrange("p (c n) -> p c n", c=4)
        WS32 = SC[:, 0:2048].bitcast(F32).rearrange("p (h s) -> p h s", h=8)
        WSb = SC[:, 2048:3072].rearrange("p (h s) -> p h s", h=8)
        CW = SC[:, 8192:9216].bitcast(F32)
        ROWS = SC[:, 9216:9472].bitcast(F32)
        PTb = [P[6][:].bitcast(BF16), P[7][:].bitcast(BF16), P[0][:].bitcast(BF16), P[1][:].bitcast(BF16)]

        PE = Q(nc, nc.tensor, st, "s_pe")
        ACT = Q(nc, nc.scalar, st, "s_act")
        DVE = Q(nc, nc.vector, st, "s_dve")
        POOL = Q(nc, nc.gpsimd, st, "s_pool")
        SP = Q(nc, nc.sync, st, "s_sp")
        par = DmaSem(nc, st, "d_par")
        ring_sems = [DmaSem(nc, st, f"d_ring{i}") for i in range(R)]
        xl = [DmaSem(nc, st, f"d_xl{t}") for t in range(8)]
        xs = [DmaSem(nc, st, f"d_xs{t}") for t in range(8)]

        st_pos = [0]

        def stc(k):
            if st_pos[0] + k > 256:
                st_pos[0] = 0
            a = ST[:, st_pos[0]:st_pos[0] + k]
            st_pos[0] += k
            return a

        V = nc.vector
        A = nc.scalar
        T = nc.tensor
        G = nc.gpsimd
        S = nc.sync

        def chain(src, k, scale, eps, waits):
            v, xh, y0, y1, y2 = stc(k), stc(k), stc(k), stc(k), stc(k)
            DVE.wait(waits)
            t = DVE.done(V.tensor_scalar(out=v, in0=src, scalar1=scale, scalar2=eps, op0=ALU.mult, op1=ALU.add))
            V.tensor_scalar(out=xh, in0=src, scalar1=-0.5 * scale, scalar2=-0.5 * eps, op0=ALU.mult, op1=ALU.add)
            DVE.wait(t)
            t = DVE.done(V.tensor_scalar(out=y0.bitcast(I32), in0=v.bitcast(I32), scalar1=-0.5, scalar2=MAGIC,
                                         op0=ALU.mult, op1=ALU.add))
            cur = y0
            for nxt in (y1, y2):
                a = stc(k)
                DVE.wait(t)
                if k == 1:
                    t = DVE.done(V.scalar_tensor_tensor(out=a, in0=cur, scalar=xh, in1=cur, op0=ALU.mult, op1=ALU.mult))
                else:
                    b = stc(k)
                    t = DVE.done(V.tensor_tensor(out=b, in0=cur, in1=cur, op=ALU.mult))
                    DVE.wait(t)
                    t = DVE.done(V.tensor_tensor(out=a, in0=b, in1=xh, op=ALU.mult))
                DVE.wait(t)
                t = DVE.done(V.scalar_tensor_tensor(out=nxt, in0=a, scalar=1.5, in1=cur, op0=ALU.add, op1=ALU.mult))
                cur = nxt
            return cur, t

        def pchain(src, scale, eps, waits, mean=None):
            v, y = stc(1), stc(1)
            POOL.wait(waits)
            t = POOL.done(G.tensor_scalar(out=v, in0=src, scalar1=scale, scalar2=eps, op0=ALU.mult, op1=ALU.add))
            POOL.wait(t)
            t = POOL.done(G.tensor_tensor(out=y, in0=v, in1=NH[:, 0:1], op=ALU.pow))
            if mean is None:
                return y, t
            nmr, nm = stc(1), stc(1)
            tn = POOL.done(G.tensor_scalar(out=nm, in0=mean, scalar1=-1.0, scalar2=0.0, op0=ALU.mult, op1=ALU.add))
            POOL.wait(t, tn)
            t = POOL.done(G.tensor_tensor(out=nmr, in0=nm, in1=y, op=ALU.mult))
            return y, nmr, t

        plan = []

        def v3(off, a, b):
            return lambda s: s[:, off:off + a * b].rearrange("p (a b) -> p a b", a=a)

        def kp(ap):
            return ap.rearrange("(k p) n -> p k n", p=128)

        def plan_ada(sub):
            base = 3 * sub * D
            for c0 in (base, base + 512, base + 1024, base + 1536, base + 2048, base + 2560):
                plan.append([(v3(0, 8, 512), kp(w_ada[:, c0:c0 + 512]))])

        def plan_ffn(sub, inserts=None):
            wi, wo = w_in[sub], w_out[sub]
            for q in range(11):
                plan.append([(v3(0, 8, 256), kp(wi[:, 256 * q:256 * q + 256])),
                             (v3(2048, 8, 256), kp(wi[:, DFF + 256 * q:DFF + 256 * q + 256]))])
                if inserts and q in inserts:
                    plan.append(inserts[q])
            for r in range(6):
                nf = 4 if r < 5 else 2
                plan.append([(v3(0, nf, 1024), kp(wo[512 * r:512 * r + 128 * nf, :]))])

        def plan_mix():
            for c0 in (1024, 1536, 0, 512):
                plan.append([(v3(0, 8, 512), kp(w_mix_in[:, c0:c0 + 512]))])
            for r in range(2):
                plan.append([(v3(0, 4, 1024), kp(w_mix_out[512 * r:512 * r + 512, :]))])

        plan_ada(0)
        gate0 = [plan.pop(), plan.pop()][::-1]
        for blk in range(nblk):
            plan_ffn(0, inserts={1: gate0[0], 2: gate0[1]} if blk == 0 else None)
            if nsub >= 2:
                if blk == 0:
                    plan_ada(1)
                plan_mix()
            if nsub >= 3:
                if blk == 0:
                    plan_ada(2)
                plan_ffn(2)

        ring = {"issued": 0, "next": 0, "tok": {}, "slot": {}, "free": list(range(R)), "rel": [None] * R}

        def pump():
            while ring["issued"] < len(plan) and ring["free"]:
                n = ring["issued"]
                s_i = ring["free"].pop(0)
                POOL.wait(ring["rel"][s_i])
                tok = None
                for fn, src in plan[n]:
                    tok = ring_sems[s_i].add(G.dma_start(out=fn(slots[s_i]), in_=src))
                ring["tok"][n] = tok
                ring["slot"][n] = s_i
                ring["issued"] += 1

        def nextchunk():
            n = ring["next"]
            ring["next"] += 1
            assert n < ring["issued"], "chunk not issued yet (ring too small)"
            return n, slots[ring["slot"][n]], ring["tok"][n]

        def release(n, tok):
            s_i = ring["slot"][n]
            ring["rel"][s_i] = tok
            ring["free"].append(s_i)
            pump()

        def gp(inst):
            t_ = POOL.done(inst)
            POOL.wait(t_)
            return t_

        pump()
        gp(G.memset(NH[:], -0.5))
        gp(G.memset(identf[:], 0.0))
        gp(G.affine_select(out=identf[:], in_=identf[:], pattern=[[-1, 128]], compare_op=ALU.not_equal, fill=1.0,
                           base=0, channel_multiplier=1))
        gp(G.memset(SEL[:], 0.0))
        for s_ in range(2):
            for j in range(3):
                gp(G.affine_select(out=SEL[:, s_, :], in_=SEL[:, s_, :], pattern=[[0, 128]], compare_op=ALU.not_equal,
                                   fill=1.0, base=-(32 * j + s_), channel_multiplier=1))
        gp(G.memset(IND[:], 1.0))
        gp(G.affine_select(out=IND[:], in_=IND[:], pattern=[[1, 512]], compare_op=ALU.is_ge, fill=0.0, base=0,
                           channel_multiplier=-64))
        tok_gconst = gp(G.affine_select(out=IND[:], in_=IND[:], pattern=[[-1, 512]], compare_op=ALU.is_ge,
                                        fill=0.0, base=63, channel_multiplier=64))
        def r8(v_):
            return v_.rearrange("(a b) -> a b", b=128)

        rows_src = [g_pre[0], g_pre[1], g_pre[2],
                    b_ada[0:1024], b_ada[1024:2048], b_ada[3072:4096], b_ada[4096:5120], b_ada[6144:7168],
                    b_ada[7168:8192]]
        r0 = 0
        for v_ in rows_src:
            par.add(S.dma_start(out=ROWS[r0:r0 + 8, :], in_=r8(v_)))
            r0 += 8
        for v_ in (conv_b, g_oa, g_ob):
            par.add(S.dma_start(out=ROWS[r0:r0 + 4, :], in_=r8(v_)))
            r0 += 4
        for s_ in range(2):
            par.add(S.dma_start(out=ROWS[r0:r0 + 8, :], in_=r8(c_in[s_, :])))
            r0 += 8
        NROWS = r0
        C_GPRE, C_BADA, C_CONVB, C_GOA, C_GOB, C_C = 0, 24, 72, 76, 80, 84
        par.add(S.dma_start(out=CW[0:31, :], in_=conv_w))
        par.add(S.dma_start(out=WS32, in_=w_sp.rearrange("h t s -> t h s")))
        par.add(S.dma_start(out=bsp[:], in_=b_sp))
        for i, v_ in enumerate((gmlp_g, gmlp_b, cn_g, cn_b)):
            par.add(S.dma_start(out=GB[:, i, :], in_=v_.partition_broadcast(128)))
        for j in range(3):
            for s_ in range(2):
                p0 = 32 * j + s_
                par.add(S.dma_start(out=GBP[p0:p0 + 1, 0, :],
                                    in_=b_ada[(3 * j + 2) * D:(3 * j + 3) * D].rearrange("(a n) -> a n", a=1)))
                par.add(S.dma_start(out=GBP[p0:p0 + 1, 1, :], in_=g_post[j].rearrange("(a n) -> a n", a=1)))
        tok_par = par.tok()

        x_ready = [None] * 8
        for t in range(8):
            x_ready[t] = xl[t].add(S.dma_start(out=xres[:, t, :], in_=x_in[t * 128:(t + 1) * 128, :]))

        pump()

        DVE.wait(tok_gconst)
        tok_ident = DVE.done(V.tensor_copy(ident[:], identf[:]))
        POOL.wait(tok_par)
        tok_mask = POOL.done(G.affine_select(out=WS32, in_=WS32, pattern=[[0, 8], [-1, 128]], compare_op=ALU.is_ge,
                                             fill=0.0, base=0, channel_multiplier=1))
        DVE.wait(tok_mask)
        tok_wsb = DVE.done(V.tensor_copy(WSb, WS32))
        PE.wait(tok_wsb, tok_ident)
        for h in range(8):
            i_ = T.transpose(PTb[0][:, h * 128:(h + 1) * 128], WSb[:, h, :], ident[:])
        tok_pe = PE.done(i_)
        DVE.wait(tok_pe)
        tok_wst = DVE.done(V.tensor_copy(WsT[:].rearrange("p h t -> p (h t)"), PTb[0][:, 0:1024]))
        PE.wait(tok_par, tok_gconst)
        i_ = T.transpose(P[0][:, 0:NROWS], ROWS[0:NROWS, :], identf[0:NROWS, 0:NROWS])
        for c in range(4):
            i_ = T.transpose(P[1][:, c * 31:(c + 1) * 31], CW[0:31, c * 128:(c + 1) * 128], identf[0:31, 0:31])
        tok_pe = PE.done(i_)
        DVE.wait(tok_pe)
        V.tensor_copy(COLS[:, 0:NROWS], P[0][:, 0:NROWS])
        tok_cols = DVE.done(V.tensor_scalar(out=convw[:].rearrange("p c k -> p (c k)"), in0=P[1][:, 0:124],
                                            scalar1=0.5, scalar2=None, op0=ALU.mult))
        for j in (0, 2):
            DVE.wait(tok_par)
            tok_gbp = DVE.done(V.tensor_scalar(out=GBP[32 * j:32 * j + 2, 1, :], in0=GBP[32 * j:32 * j + 2, 1, :],
                                               scalar1=0.5, scalar2=None, op0=ALU.mult))
        ACT.wait(tok_cols)
        for s_ in range(2):
            i_ = A.activation(out=cTb[:, :, s_], in_=COLS[:, C_C + 8 * s_:C_C + 8 * s_ + 8], func=AF.Silu)
        tok_ct = ACT.done(i_)

        state = {"scratch_free": [tok_pe, tok_wst, tok_cols], "yset_free": [None, None], "gu_free": [tok_cols, None],
                 "pt_free": [tok_wst, None, None, None], "presq": None, "ssx_idx": 0, "mod_tok": [None, None, None], "gr_tok": [None, None, None],
                 "gg_free": None, "tmp_free": None, "sg_free": [None, None], "ht_free": None,
                 "pending_ada": 1, "pending_ada2": True}

        def ada_steps(sub, pa, pg, bank_wait):
            adav = ADA[:, sub * 16:(sub + 1) * 16, :]
            pav = pa[:, 0:32].rearrange("p (f s) -> p f s", s=2)
            p0 = 32 * sub
            loc = {"t1": None, "free": []}

            def feat(k):
                vi, hc = k // 2, k % 2
                n, slot, tk = nextchunk()
                w = slot[:, :].rearrange("p (k n) -> p k n", k=8)
                PE.wait(tk, tok_ct, bank_wait)
                for f4 in range(4):
                    fc = hc * 4 + f4
                    col = (vi * 8 + fc) * 2
                    for kc in range(8):
                        i_ = T.matmul(pa[:, col:col + 2], lhsT=w[:, kc, f4 * 128:(f4 + 1) * 128], rhs=cTb[:, kc, :],
                                      start=(kc == 0), stop=(kc == 7))
                last = PE.done(i_)
                release(n, last)
                if k < 3:
                    return
                DVE.wait(last, tok_cols)
                for vi_ in range(2):
                    bcol = C_BADA + (2 * sub + vi_) * 8
                    for s_ in range(2):
                        i_ = V.tensor_tensor(out=adav[:, vi_ * 8:(vi_ + 1) * 8, s_], in0=pav[:, vi_ * 8:(vi_ + 1) * 8, s_],
                                             in1=COLS[:, bcol:bcol + 8], op=ALU.add)
                t1 = DVE.done(i_)
                loc["free"].append(t1)
                DVE.wait(t1)
                for s_ in range(2):
                    V.tensor_copy(MOD[:, sub, 1, :, s_], adav[:, 0:8, s_])
                    i_ = V.scalar_tensor_tensor(out=MOD[:, sub, 0, :, s_], in0=adav[:, 8:16, s_], scalar=1.0,
                                                in1=COLS[:, C_GPRE + 8 * sub:C_GPRE + 8 * sub + 8], op0=ALU.add,
                                                op1=ALU.mult)
                state["mod_tok"][sub] = DVE.done(i_)

            def gate(hc):
                n, slot, tk = nextchunk()
                w = slot[:, :].rearrange("p (k n) -> p k n", k=8)
                PE.wait(tk, tok_ct, bank_wait, loc["t1"])
                for kc in range(8):
                    i_ = T.matmul(pg[p0:p0 + 2, :], lhsT=cTb[:, kc, :], rhs=w[:, kc, :], start=(kc == 0), stop=(kc == 7))
                tpe = PE.done(i_)
                release(n, tpe)
                DVE.wait(tpe, tok_par, tok_gbp)
                grv = GR[p0:p0 + 2, hc * 512:(hc + 1) * 512]
                t1 = DVE.done(V.tensor_tensor(out=grv, in0=pg[p0:p0 + 2, :], in1=GBP[p0:p0 + 2, 0, hc * 512:(hc + 1) * 512],
                                              op=ALU.add))
                loc["t1"] = t1
                DVE.wait(t1)
                t2 = DVE.done(V.tensor_tensor(out=grv, in0=grv, in1=GBP[p0:p0 + 2, 1, hc * 512:(hc + 1) * 512],
                                              op=ALU.mult))
                if hc == 1:
                    state["gr_tok"][sub] = t2
                    loc["free"].append(t2)

            steps = [lambda k=k: feat(k) for k in range(4)] + [lambda h=h: gate(h) for h in range(2)]
            return steps, loc

        def ada(sub):
            steps, loc = ada_steps(sub, P[0], P[1], [state["gu_free"][0]])
            for st_ in steps:
                st_()
            state["gu_free"][0] = loc["free"]

        def presquare(t):
            if state["presq"] is None:
                state["ssx_idx"] ^= 1
                state["presq"] = {}
            col = 8 * state["ssx_idx"] + t
            ACT.wait(x_ready[t], state["ht_free"])
            state["presq"][t] = ACT.done(A.activation(out=hT[:, t, :], in_=xres[:, t, :], func=AF.Square,
                                                      accum_out=SSX[:, col:col + 1]))

        def prenorm(sub, s_):
            for t in range(4):
                if state["presq"] is None or t not in state["presq"]:
                    presquare(t)
            ssx = SSX[:, 8 * state["ssx_idx"]:8 * state["ssx_idx"] + 8]
            rs = [chain(ssx[:, 0:4], 4, 1.0 / D, EPS, [state["presq"][3]]), None]
            ht_ready = []
            xn_free = state["scratch_free"]
            tpe = None
            for half in range(2):
                rstd4, tR = rs[half]
                ACT.wait(tR, xn_free)
                DVE.wait(tR, xn_free)
                ia = iv = None
                for tt in range(4):
                    t = half * 4 + tt
                    if tt % 2 == 0:
                        ia = A.activation(out=xn[:, tt, :], in_=xres[:, t, :], func=AF.Identity, scale=rstd4[:, tt:tt + 1])
                    else:
                        iv = V.tensor_scalar(out=xn[:, tt, :], in0=xres[:, t, :], scalar1=rstd4[:, tt:tt + 1], scalar2=None,
                                             op0=ALU.mult)
                t_xn = [ACT.done(ia), DVE.done(iv)]
                if half == 0:
                    for t in range(4, 8):
                        if t not in state["presq"]:
                            presquare(t)
                    rs[1] = chain(ssx[:, 4:8], 4, 1.0 / D, EPS, [state["presq"][7]])
                    state["presq"] = None
                evs = []
                for kc in range(8):
                    pb = kc % 4
                    PE.wait(t_xn, state["pt_free"][pb], tok_ident, state["yset_free"][1], state["gu_free"][0])
                    for tt in range(4):
                        i_ = T.transpose(PTb[pb][:, tt * 128:(tt + 1) * 128], xn[:, tt, kc * 128:(kc + 1) * 128], ident[:])
                    tpe = PE.done(i_)
                    gm = MOD[:, sub, 0, kc, s_:s_ + 1]
                    sh = MOD[:, sub, 1, kc, s_:s_ + 1]
                    dst = hT[:, kc, half * 512:(half + 1) * 512]
                    if pb % 2 == 0:
                        ACT.wait(tpe, state["mod_tok"][sub])
                        te = ACT.done(A.activation(out=dst, in_=PTb[pb][:, 0:512], func=AF.Identity, scale=gm, bias=sh))
                    else:
                        DVE.wait(tpe, state["mod_tok"][sub])
                        te = DVE.done(V.tensor_scalar(out=dst, in0=PTb[pb][:, 0:512], scalar1=gm, scalar2=sh,
                                                      op0=ALU.mult, op1=ALU.add))
                    state["pt_free"][pb] = te
                    evs.append(te)
                xn_free = [tpe]
                ht_ready.append(evs[-4:])
            state["gu_free"][0] = [state["gu_free"][0], state["pt_free"][2], state["pt_free"][3]]
            return ht_ready, tpe

        def make_gg(sub, s_, banks=None, waits=None):
            p0 = 32 * sub
            tks = []
            for half in range(2):
                bank = P[4 + half] if banks is None else banks[half]
                PE.wait(state["gr_tok"][sub], state["yset_free"][0], tok_gconst, waits)
                tpe = PE.done(T.matmul(bank[:, :], lhsT=SEL[p0:p0 + 2, s_, :], rhs=GR[p0:p0 + 2, half * 512:(half + 1) * 512],
                                       start=True, stop=True))
                ACT.wait(tpe, state["gg_free"])
                tks.append(ACT.done(A.activation(out=gg[:, half * 512:(half + 1) * 512], in_=bank[:, :], func=AF.Identity)))
            if banks is None:
                state["yset_free"][0] = [state["yset_free"][0], tks[-1]]
            return tks[-1]

        def epilogue(t, yset, tpe, tok_gg, banks=None):
            bA, bB = (P[4 + 2 * yset], P[5 + 2 * yset]) if banks is None else banks
            ss2 = stc(2)
            ACT.wait(tpe, state["ht_free"])
            A.activation(out=hT[:, 0, 0:512], in_=bA[:, :], func=AF.Square, accum_out=ss2[:, 0:1])
            tA = ACT.done(A.activation(out=hT[:, 0, 512:1024], in_=bB[:, :], func=AF.Square, accum_out=ss2[:, 1:2]))
            ss = stc(1)
            POOL.wait(tA)
            t1 = POOL.done(G.tensor_tensor(out=ss, in0=ss2[:, 0:1], in1=ss2[:, 1:2], op=ALU.add))
            rstd, tR = pchain(ss, 1.0 / D, EPS, [t1])
            tmp = T32[:, 2:4, :]
            DVE.wait(tpe, tok_gg, state["tmp_free"])
            DVE.wait(tA)
            V.tensor_tensor(out=tmp[:, 0, :], in0=gg[:, 0:512], in1=bA[:, :], op=ALU.mult)
            t_tmp = DVE.done(V.tensor_tensor(out=tmp[:, 1, :], in0=gg[:, 512:1024], in1=bB[:, :], op=ALU.mult))
            if banks is None:
                state["yset_free"][yset] = [tA, t_tmp]
            DVE.wait(tR, t_tmp)
            tX = DVE.done(V.scalar_tensor_tensor(out=xres[:, t, :], in0=tmp.rearrange("p a n -> p (a n)"), scalar=rstd,
                                                 in1=xres[:, t, :], op0=ALU.mult, op1=ALU.add))
            state["tmp_free"] = tX
            state["gg_free"] = tX
            x_ready[t] = tX
            return [tA, t_tmp]

        def ffn(sub, s_):
            last_sub = (sub + 1 == nsub) or (sub == 2)
            ht_ready, _ = prenorm(sub, s_)
            it = 0
            for q in range(11):
                n, slot, tk = nextchunk()
                wg = slot[:, 0:2048].rearrange("p (k n) -> p k n", k=8)
                wu = slot[:, 2048:4096].rearrange("p (k n) -> p k n", k=8)
                tpe = None
                for sf in range(2):
                    fc = 2 * q + sf
                    for half in range(2):
                        ps = it % 2
                        bg, bu = P[2 * ps], P[2 * ps + 1]
                        PE.wait(tk, ht_ready[half], state["gu_free"][ps])
                        rhs_cols = slice(half * 512, (half + 1) * 512)
                        for kc in range(8):
                            T.matmul(bg[:, :], lhsT=wg[:, kc, sf * 128:(sf + 1) * 128], rhs=hT[:, kc, rhs_cols],
                                     start=(kc == 0), stop=(kc == 7))
                        for kc in range(8):
                            i_ = T.matmul(bu[:, :], lhsT=wu[:, kc, sf * 128:(sf + 1) * 128], rhs=hT[:, kc, rhs_cols],
                                          start=(kc == 0), stop=(kc == 7))
                        tpe = PE.done(i_)
                        sg = T32[:, ps, :]
                        ACT.wait(tpe, state["sg_free"][ps])
                        tA = ACT.done(A.activation(out=sg, in_=bg[:, :], func=AF.Silu))
                        DVE.wait(tA, tpe)
                        tD = DVE.done(V.tensor_tensor(out=actT[:, fc, rhs_cols], in0=sg, in1=bu[:, :], op=ALU.mult))
                        state["gu_free"][ps] = tD
                        state["sg_free"][ps] = tD
                        it += 1
                release(n, tpe)
                if sub == 0 and state["gate0_steps"] is not None and q in (1, 2):
                    state["gate0_steps"][0][3 + q]()
                    if q == 2:
                        state["yset_free"][0] = [state["yset_free"][0], state["gate0_steps"][1]["free"]]
                        state["gate0_steps"] = None
            act_ready = tD
            state["ht_free"] = tpe
            w2 = []
            for r in range(6):
                n, slot, tk = nextchunk()
                w2.append((n, slot[:, :].rearrange("p (f n) -> p f n", f=4), tk))
            tok_gg = make_gg(sub, s_)
            tpe = None
            side = None
            if state["pending_ada"] is not None and sub == 0 and nsub >= 2:
                side = ada_steps(state["pending_ada"], P[0], P[1], [state["gu_free"][0], state["gu_free"][1]])
                state["pending_ada"] = None
            for t in range(8):
                ys = t % 2
                bA, bB = P[4 + 2 * ys], P[5 + 2 * ys]
                PE.wait(state["yset_free"][ys], act_ready, [w[2] for w in w2], state["pt_free"])
                for fc in range(22):
                    wv = w2[fc // 4][1]
                    lhs = actT[:, fc, t * 128:(t + 1) * 128]
                    T.matmul(bA[:, :], lhsT=lhs, rhs=wv[:, fc % 4, 0:512], start=(fc == 0), stop=(fc == 21))
                    i_ = T.matmul(bB[:, :], lhsT=lhs, rhs=wv[:, fc % 4, 512:1024], start=(fc == 0), stop=(fc == 21))
                tpe = PE.done(i_)
                epilogue(t, ys, tpe, tok_gg)
                if side is not None and t < 6:
                    side[0][t]()
                if not last_sub:
                    if t >= 1:
                        presquare(t - 1)
            for (n, _, _) in w2:
                release(n, tpe)
            if side is not None:
                state["gu_free"][0] = [state["gu_free"][0], side[1]["free"]]
            state["scratch_free"] = [tpe]

        def tail_front(src, kind, par_, waits):
            ss = stc(1)
            ACT.wait(waits, state["ynb_free"][kind][par_])
            tA = ACT.done(A.activation(out=ynb[:, kind, par_, :], in_=src, func=AF.Square, accum_out=ss))
            rstd, tR = pchain(ss, 1.0 / 512, EPS, [tA])
            ACT.wait(tR)
            return ACT.done(A.activation(out=ynb[:, kind, par_, :], in_=src, func=AF.Identity, scale=rstd))

        def tail_back(tY, kind, par_, t, cbase, gcol):
            PE.wait(tY, state["pt_free"][kind], state["yset_free"][1])
            for c in range(4):
                i_ = T.transpose(PTb[kind][:, c * 128:(c + 1) * 128], ynb[:, kind, par_, c * 128:(c + 1) * 128], ident[:])
            tpe = PE.done(i_)
            state["ynb_free"][kind][par_] = tpe
            te = None
            for c in range(4):
                dst = yT[:, cbase + c, t * 128:(t + 1) * 128]
                srcp = PTb[kind][:, c * 128:(c + 1) * 128]
                gcl = COLS[:, gcol + c:gcol + c + 1]
                if kind == 0:
                    ACT.wait(tpe)
                    te = ACT.done(A.activation(out=dst, in_=srcp, func=AF.Identity, scale=gcl))
                else:
                    DVE.wait(tpe)
                    te = DVE.done(V.tensor_scalar(out=dst, in0=srcp, scalar1=gcl, scalar2=None, op0=ALU.mult))
            state["pt_free"][kind] = te
            return te

        def lnorm_a(psrc, dst, gi, waits, dst_free=None):
            st6 = stc(6)
            mv = stc(2)
            DVE.wait(waits)
            t1 = DVE.done(V.bn_stats(st6, psrc))
            DVE.wait(t1)
            t2 = DVE.done(V.bn_aggr(mv, st6))
            rstd, tR = pchain(mv[:, 1:2], 1.0, EPS, [t2])
            DVE.wait(t2, dst_free)
            tA_ = DVE.done(V.scalar_tensor_tensor(out=dst, in0=psrc, scalar=mv[:, 0:1], in1=GB[:, gi, :],
                                                  op0=ALU.subtract, op1=ALU.mult))
            return rstd, tR, tA_

        def mixer(blk, s_):
            sub = 1
            ht_ready, t_tr = prenorm(sub, s_)
            state.setdefault("ynb_free", [[None, None], [None, None]])
            DVE.wait(t_tr, state["scratch_free"])
            if blk % 2 == 0:
                t_halo = DVE.done(V.memset(glu[:, :, 0:30], 0.0))
            else:
                t_halo = DVE.done(V.tensor_copy(glu[:, :, 0:30], HALO[:]))
            dg_free = [t_tr, t_tr]
            taps = [range(0, 16), range(16, 31)]

            def dg_gen(u):
                c_, part_ = u // 2, u % 2
                DVE.wait(dg_free[part_], tok_cols)
                for k in taps[part_]:
                    i_ = V.tensor_scalar(out=DG[:, k, :], in0=ident[:], scalar1=convw[:, c_, k:k + 1], scalar2=None,
                                         op0=ALU.mult)
                return DVE.done(i_)

            t_dgs = {0: dg_gen(0)}
            nA, slotA, tkA = nextchunk()
            nG, slotG, tkG = nextchunk()
            wA = slotA[:, :].rearrange("p (k n) -> p k n", k=8)
            wG = slotG[:, :].rearrange("p (k n) -> p k n", k=8)
            it = 0
            for c in range(4):
                for half in range(2):
                    ps = it % 2
                    ba, bg = P[2 * ps], P[2 * ps + 1]
                    cols = slice(half * 512, (half + 1) * 512)
                    PE.wait(tkA, tkG, ht_ready[half], state["gu_free"][ps])
                    for kc in range(8):
                        T.matmul(ba[:, :], lhsT=wA[:, kc, c * 128:(c + 1) * 128], rhs=hT[:, kc, cols], start=(kc == 0),
                                 stop=(kc == 7))
                    for kc in range(8):
                        i_ = T.matmul(bg[:, :], lhsT=wG[:, kc, c * 128:(c + 1) * 128], rhs=hT[:, kc, cols],
                                      start=(kc == 0), stop=(kc == 7))
                    tpe = PE.done(i_)
                    tg = T32[:, ps, :]
                    ACT.wait(tpe, state["sg_free"][ps])
                    tA = ACT.done(A.activation(out=tg, in_=bg[:, :], func=AF.Tanh, scale=0.5))
                    DVE.wait(tA, tpe, t_halo)
                    tD = DVE.done(V.scalar_tensor_tensor(out=glu[:, c, 30 + half * 512:30 + (half + 1) * 512], in0=tg,
                                                         scalar=1.0, in1=ba[:, :], op0=ALU.add, op1=ALU.mult))
                    state["gu_free"][ps] = tD
                    state["sg_free"][ps] = tD
                    it += 1
            release(nA, tpe)
            release(nG, tpe)
            DVE.wait(tD)
            t_glu = DVE.done(V.tensor_copy(HALO[:], glu[:, :, 1024:1054]))
            nU, slotU, tkU = nextchunk()
            nV, slotV, tkV = nextchunk()
            wU = slotU[:, :].rearrange("p (k n) -> p k n", k=8)
            wV = slotV[:, :].rearrange("p (k n) -> p k n", k=8)
            uv_free = [state["gu_free"][0], state["gu_free"][1]]
            us_free = [state["sg_free"][0], state["sg_free"][1]]
            vn_free = [None, None]
            sh_ = {"z_free": [state["yset_free"][1], state["pt_free"][1]], "cvt_free": None, "t_uv": None, "t_cv": None}
            tk = {}
            uS = [T32[:, 0, :], T32[:, 1, :]]
            tbA = [T32[:, 2, :], T32[:, 3, :]]
            tbB = [T32[:, 3, :], T32[:, 4, :], T32[:, 5, :]]

            def A1(t):
                p_ = t % 2
                bu, bv = P[2 * p_], P[2 * p_ + 1]
                tcols = slice(t * 128, (t + 1) * 128)
                PE.wait(tkU, tkV, ht_ready[0], ht_ready[1], uv_free[p_])
                for kc in range(8):
                    T.matmul(bu[:, :], lhsT=hT[:, kc, tcols], rhs=wU[:, kc, :], start=(kc == 0), stop=(kc == 7))
                for kc in range(8):
                    i_ = T.matmul(bv[:, :], lhsT=hT[:, kc, tcols], rhs=wV[:, kc, :], start=(kc == 0), stop=(kc == 7))
                t_uv = PE.done(i_)
                sh_["t_uv"] = t_uv
                ACT.wait(t_uv, us_free[p_])
                tk["u", t] = ACT.done(A.activation(out=uS[p_], in_=bu[:, :], func=AF.Identity))
                rstd, tR, tA_ = lnorm_a(bv[:, :], tbA[p_], 0, [t_uv], dst_free=tk.get(("yA", t - 2)))
                uv_free[p_] = [tk["u", t], tA_]
                DVE.wait(vn_free[p_], tR, tA_)
                tk["vn", t] = DVE.done(V.scalar_tensor_tensor(out=vn[:, p_, :], in0=tbA[p_], scalar=rstd, in1=GB[:, 1, :],
                                                              op0=ALU.mult, op1=ALU.add))

            def A2(t):
                p_ = t % 2
                PE.wait(tk["vn", t], sh_["z_free"], tok_wst, tok_par, tok_gconst)
                T.matmul(P[7][:, :], lhsT=bsp[0:8, :], rhs=IND[0:8, :], start=True, stop=False)
                for h in range(8):
                    i_ = T.matmul(P[7][:, h * 64:(h + 1) * 64], lhsT=WsT[:, h, :], rhs=vn[:, p_, h * 64:(h + 1) * 64],
                                  start=False, stop=(h == 7))
                t_z = PE.done(i_)
                vn_free[p_] = t_z
                DVE.wait(t_z, tk["u", t])
                t_ya = DVE.done(V.tensor_tensor(out=tbA[p_], in0=uS[p_], in1=P[7][:, :], op=ALU.mult))
                sh_["z_free"] = t_ya
                us_free[p_] = t_ya
                tk["yA", t] = tail_front(tbA[p_], 0, p_, [t_ya])

            cv_free = [state["yset_free"][0], state["yset_free"][0]]
            t_cv = None
            evA = evB = None
            for unit in range(8):
                c, part = unit // 2, unit % 2
                if unit + 1 < 8:
                    t_dgs[unit + 1] = dg_gen(unit + 1)
                t_dg = t_dgs[unit]
                for half in range(2):
                    bank = P[4 + half]
                    PE.wait(t_dg, t_glu, cv_free[half])
                    for k in taps[part]:
                        i_ = T.matmul(bank[:, :], lhsT=DG[:, k, :], rhs=glu[:, c, half * 512 + k:half * 512 + k + 512],
                                      start=(k == 0), stop=(k == 30))
                    tpe = PE.done(i_)
                    if part == 1:
                        ACT.wait(tpe)
                        t_cv = ACT.done(A.activation(out=convS[:, c, half * 512:(half + 1) * 512], in_=bank[:, :],
                                                     func=AF.Identity, bias=COLS[:, C_CONVB + c:C_CONVB + c + 1]))
                        cv_free[half] = t_cv
                dg_free[part] = tpe
                A1(unit)
                if unit >= 1:
                    A2(unit - 1)
                if unit >= 2:
                    evA = tail_back(tk["yA", unit - 2], 0, unit % 2, unit - 2, 0, C_GOA)
            A2(7)
            evA = tail_back(tk["yA", 6], 0, 0, 6, 0, C_GOA)
            evA = tail_back(tk["yA", 7], 0, 1, 7, 0, C_GOA)
            sh_["t_cv"] = t_cv
            sh_["cvt_free"] = [cv_free[0], cv_free[1]]
            state["pt_free"][1] = [state["pt_free"][1], sh_["z_free"]]
            def B_s0(t):
                p_ = t % 2
                tcols = slice(t * 128, (t + 1) * 128)
                PE.wait(sh_["t_cv"], sh_["cvt_free"])
                for c in range(4):
                    i_ = T.transpose(P[5][:, c * 128:(c + 1) * 128], convS[:, c, tcols], identf[:])
                t_cvt = PE.done(i_)
                rstd, tR, tA_ = lnorm_a(P[5][:, :], tbB[t % 3], 2, [t_cvt], dst_free=tk.get(("yB", t - 3)))
                sh_["cvt_free"] = tA_
                tk["lnB", t] = (rstd, tR, tA_)

            def B_s1(t):
                p_ = t % 2
                rstd, tR, tA_ = tk["lnB", t]
                DVE.wait(tR, tA_)
                tb_ = tbB[t % 3]
                t5 = DVE.done(V.scalar_tensor_tensor(out=tb_, in0=tb_, scalar=rstd, in1=GB[:, 3, :],
                                                     op0=ALU.mult, op1=ALU.add))
                ACT.wait(t5)
                t_yb = ACT.done(A.activation(out=tb_, in_=tb_, func=AF.Silu))
                ss = stc(1)
                ACT.wait(t_yb, state["ynb_free"][1][p_])
                tA = ACT.done(A.activation(out=ynb[:, 1, p_, :], in_=tb_, func=AF.Square, accum_out=ss))
                tk["sqB", t] = (ss, tA)

            def B_s2(t):
                p_ = t % 2
                ss, tA = tk["sqB", t]
                rstd, tR = pchain(ss, 1.0 / 512, EPS, [tA])
                ACT.wait(tR)
                tk["yB", t] = ACT.done(A.activation(out=ynb[:, 1, p_, :], in_=tbB[t % 3], func=AF.Identity, scale=rstd))

            release(nU, sh_["t_uv"])
            release(nV, sh_["t_uv"])
            nO0, slotO0, tkO0 = nextchunk()
            nO1, slotO1, tkO1 = nextchunk()
            wO = [slotO0[:, :].rearrange("p (k n) -> p k n", k=4), slotO1[:, :].rearrange("p (k n) -> p k n", k=4)]
            tok_gg = make_gg(sub, s_, banks=[P[4], P[6]], waits=[evA, sh_["cvt_free"], state["pt_free"][0]])
            state["tmp_free"] = [evA]
            yfree = [uv_free[0], uv_free[1]]
            evBs = {}
            o_tpe = {}
            last_pe = [None]

            def M4_mm(t):
                ys = t % 2
                bA, bB = P[2 * ys], P[2 * ys + 1]
                PE.wait(yfree[ys], evA, evBs[t], tkO0, tkO1)
                for kc in range(8):
                    lhs = yT[:, kc, t * 128:(t + 1) * 128]
                    wv = wO[kc // 4]
                    T.matmul(bA[:, :], lhsT=lhs, rhs=wv[:, kc % 4, 0:512], start=(kc == 0), stop=(kc == 7))
                    i_ = T.matmul(bB[:, :], lhsT=lhs, rhs=wv[:, kc % 4, 512:1024], start=(kc == 0), stop=(kc == 7))
                o_tpe[t] = PE.done(i_)
                last_pe[0] = o_tpe[t]

            TM = [T32[:, 0:2, :], T32[:, 0:2, :]]
            epi = {}

            def epi_a(t):
                ys = t % 2
                bA, bB = P[2 * ys], P[2 * ys + 1]
                tmp = TM[ys]
                DVE.wait(o_tpe[t], tok_gg, state["tmp_free"], epi.get(t - 1))
                V.tensor_tensor(out=tmp[:, 0, :], in0=gg[:, 0:512], in1=bA[:, :], op=ALU.mult)
                t_tmp = DVE.done(V.tensor_tensor(out=tmp[:, 1, :], in0=gg[:, 512:1024], in1=bB[:, :], op=ALU.mult))
                ss2 = stc(2)
                ACT.wait(t_tmp, state["ht_free"])
                A.activation(out=hT[:, 0, 0:512], in_=bA[:, :], func=AF.Square, accum_out=ss2[:, 0:1])
                tA = ACT.done(A.activation(out=hT[:, 0, 512:1024], in_=bB[:, :], func=AF.Square, accum_out=ss2[:, 1:2]))
                yfree[ys] = [tA, t_tmp]
                ss = stc(1)
                POOL.wait(tA)
                t1 = POOL.done(G.tensor_tensor(out=ss, in0=ss2[:, 0:1], in1=ss2[:, 1:2], op=ALU.add))
                rstd, tR = pchain(ss, 1.0 / D, EPS, [t1])
                tk["epi", t] = (rstd, tR, t_tmp)

            def epi_b(t):
                ys = t % 2
                rstd, tR, t_tmp = tk["epi", t]
                DVE.wait(tR, t_tmp)
                tX = DVE.done(V.scalar_tensor_tensor(out=xres[:, t, :], in0=TM[ys].rearrange("p a n -> p (a n)"),
                                                     scalar=rstd, in1=xres[:, t, :], op0=ALU.mult, op1=ALU.add))
                epi[t] = tX
                state["gg_free"] = tX
                x_ready[t] = tX

            side = None
            if state["pending_ada2"] and nsub >= 3:
                side = ada_steps(2, P[4], P[6], [tok_gg])
                state["pending_ada2"] = False
            for i in range(8 + 6):
                if side is not None and 2 <= i < 8:
                    side[0][i - 2]()
                if 0 <= i - 2 < 8:
                    B_s2(i - 2)
                if i < 8:
                    B_s0(i)
                if 0 <= i - 3 < 8:
                    t = i - 3
                    evBs[t] = tail_back(tk["yB", t], 1, t % 2, t, 4, C_GOB)
                if 0 <= i - 4 < 8:
                    M4_mm(i - 4)
                if 0 <= i - 1 < 8:
                    B_s1(i - 1)
                if 0 <= i - 5 < 8:
                    epi_b(i - 5)
                if 0 <= i - 4 < 8:
                    epi_a(i - 4)
                if nsub >= 3 and 0 <= i - 6 < 8:
                    presquare(i - 6)
            state["tmp_free"] = [epi[6], epi[7]]
            state["sg_free"] = [epi[6], epi[7]]
            us_free = [epi[6], epi[7]]
            tpe = last_pe[0]
            release(nO0, tpe)
            release(nO1, tpe)
            state["ht_free"] = sh_["t_uv"]
            state["gu_free"] = [yfree[0], yfree[1]]
            state["sg_free"] = [us_free[0], us_free[1]]
            side_free = side[1]["free"] if side is not None else None
            state["yset_free"][0] = [sh_["cvt_free"], tok_gg, side_free]
            state["yset_free"][1] = [sh_["z_free"], tok_gg, side_free]
            state["scratch_free"] = [tpe]


        steps0, loc0 = ada_steps(0, P[0], P[4], [state["gu_free"][0]])
        for k_ in range(4):
            steps0[k_]()
        state["gu_free"][0] = list(loc0["free"])
        state["gate0_steps"] = (steps0, loc0)
        for blk in range(nblk):
            s_ = blk // 2
            ffn(0, s_)
            if nsub >= 2:
                mixer(blk, s_)
            if nsub >= 3:
                ffn(2, s_)
            for t in range(8):
                SP.wait(x_ready[t])
                row0 = blk * TB + t * 128
                tk = xs[t].add(S.dma_start(out=out[row0:row0 + 128, :], in_=xres[:, t, :]))
                if blk + 1 < nblk:
                    SP.wait(tk)
                    row1 = (blk + 1) * TB + t * 128
                    x_ready[t] = xl[t].add(S.dma_start(out=xres[:, t, :], in_=x_in[row1:row1 + 128, :]))
        for t in range(8):
            SP.wait(xs[t].tok())
        assert ring["next"] == len(plan) == ring["issued"], (ring["next"], len(plan), ring["issued"])
    return nc


_W_NAMES = ["w_ada", "b_ada", "g_pre_f1", "g_post_f1", "w_f1_in", "w_f1_out", "g_pre_m", "g_post_m", "w_mix_in",
            "gmlp_norm_g", "gmlp_norm_b", "w_spatial", "b_spatial", "conv_w", "conv_b", "conv_norm_g", "conv_norm_b",
            "g_out_a", "g_out_b", "w_mix_out", "g_pre_f2", "g_post_f2", "w_f2_in", "w_f2_out"]


def make_in_maps(inputs, ncores=NCORES):
    x = np.ascontiguousarray(np.asarray(inputs["x"], dtype=np.float32))
    c = np.ascontiguousarray(np.asarray(inputs["c"], dtype=np.float32))
    shared = {k: np.ascontiguousarray(np.asarray(inputs[k], dtype=np.float32)[0]) for k in _W_NAMES}
    maps = []
    for i in range(ncores):
        m = dict(shared)
        m["x"] = x[2 * i:2 * i + 2].reshape(TOK, D)
        m["c"] = c[2 * i:2 * i + 2]
        maps.append(m)
    return maps


def kernel(**inputs):
    nc = build()
    maps = make_in_maps(inputs)
    res = run_bass_kernel_spmd(nc, maps, core_ids=list(range(NCORES)))
    outs = [np.asarray(r["out"], dtype=np.float32).reshape(2, 2048, D) for r in res.results]
    return np.concatenate(outs, axis=0)
```

```python
import contextlib
import numpy as np
import concourse.bass as bass
import concourse.mybir as mybir
from concourse.bass_utils import run_bass_kernel_spmd

F32, BF16, I32 = mybir.dt.float32, mybir.dt.bfloat16, mybir.dt.int32
AF = mybir.ActivationFunctionType
ALU = mybir.AluOpType

D = 1024
DFF = 2816
NCORES = 8
TOK = 4096
TB = 1024
NBLK = 4
EPS = 1e-6
R = 8
MAGIC = float(0x5F3759DF)


class Tok:
    __slots__ = ("sem", "val", "key")

    def __init__(self, sem, val, key):
        self.sem, self.val, self.key = sem, val, key


class Q:
    def __init__(self, nc, eng, st, name):
        self.e = eng
        self.sem = st.enter_context(nc.semaphore(name))
        self.key = name
        self.cnt = 0
        self.seen = {}

    def wait(self, *toks):
        for tok in toks:
            if tok is None:
                continue
            if isinstance(tok, (list, tuple)):
                self.wait(*tok)
                continue
            if self.seen.get(tok.key, 0) >= tok.val:
                continue
            self.e.wait_ge(tok.sem, tok.val)
            self.seen[tok.key] = tok.val

    def done(self, inst):
        self.cnt += 1
        inst.then_inc(self.sem, 1)
        return Tok(self.sem, self.cnt, self.key)


class DmaSem:
    def __init__(self, nc, st, name):
        self.sem = st.enter_context(nc.semaphore(name))
        self.key = name
        self.cnt = 0

    def add(self, inst):
        self.cnt += 16
        inst.then_inc(self.sem, 16)
        return Tok(self.sem, self.cnt, self.key)

    def tok(self):
        return Tok(self.sem, self.cnt, self.key)


def build(nsub=3, nblk=NBLK):
    nc = bass.Bass("TRN2", target_bir_lowering=False)

    def din(name, shape):
        return nc.dram_tensor(name, shape, F32, kind="ExternalInput").ap()

    x_in = din("x", [TOK, D])
    c_in = din("c", [2, D])
    w_ada = din("w_ada", [D, 9 * D])
    b_ada = din("b_ada", [9 * D])
    g_pre = [din("g_pre_f1", [D]), din("g_pre_m", [D]), din("g_pre_f2", [D])]
    g_post = [din("g_post_f1", [D]), din("g_post_m", [D]), din("g_post_f2", [D])]
    w_in = {0: din("w_f1_in", [D, 2 * DFF]), 2: din("w_f2_in", [D, 2 * DFF])}
    w_out = {0: din("w_f1_out", [DFF, D]), 2: din("w_f2_out", [DFF, D])}
    w_mix_in = din("w_mix_in", [D, 2048])
    w_mix_out = din("w_mix_out", [D, D])
    gmlp_g = din("gmlp_norm_g", [512])
    gmlp_b = din("gmlp_norm_b", [512])
    w_sp = din("w_spatial", [8, 128, 128])
    b_sp = din("b_spatial", [8, 128])
    conv_w = din("conv_w", [31, 512])
    conv_b = din("conv_b", [512])
    cn_g = din("conv_norm_g", [512])
    cn_b = din("conv_norm_b", [512])
    g_oa = din("g_out_a", [512])
    g_ob = din("g_out_b", [512])
    out = nc.dram_tensor("out", [TOK, D], F32, kind="ExternalOutput").ap()

    with contextlib.ExitStack() as st:
        E = st.enter_context

        def sb(name, shape, dt):
            return E(nc.sbuf_tensor(name, shape, dt))

        xres = sb("xres", [128, 8, 1024], F32)
        hT = sb("hT", [128, 8, 1024], BF16)
        SC = sb("SC", [128, 22528], BF16)
        slots = [sb(f"ring{i}", [128, 4096], BF16) for i in range(R)]
        GR = sb("GR", [128, 1024], F32)
        GBP = sb("GBP", [128, 2, 1024], F32)
        gg = sb("gg", [128, 1024], F32)
        GB = sb("GB", [128, 4, 512], F32)
        T32 = sb("T32", [128, 6, 512], F32)
        vn = sb("vn", [128, 2, 512], BF16)
        ynb = sb("ynb", [128, 2, 2, 512], BF16)
        ident = sb("ident", [128, 128], BF16)
        identf = sb("identf", [128, 128], F32)
        SEL = sb("SEL", [128, 2, 128], F32)
        WsT = sb("WsT", [128, 8, 128], BF16)
        IND = sb("IND", [8, 512], F32)
        bsp = sb("bsp", [8, 128], F32)
        COLS = sb("COLS", [128, 128], F32)
        convw = sb("convw", [128, 4, 31], F32)
        MOD = sb("MOD", [128, 3, 2, 8, 2], F32)
        ADA = sb("ADA", [128, 48, 2], F32)
        ST = sb("ST", [128, 256], F32)
        HALO = sb("HALO", [128, 4, 30], BF16)
        cTb = sb("cTb", [128, 8, 2], BF16)
        NH = sb("NH", [128, 8], F32)
        SSX = sb("SSX", [128, 16], F32)
        P = [E(nc.psum_tensor(f"P{i}", [128, 512], F32)) for i in range(8)]

        actT = SC[:, :].rearrange("p (f n) -> p f n", f=22)
        xn = SC[:, 18432:22528].rearrange("p (t n) -> p t n", t=4)
        yT = SC[:, 0:8192].rearrange("p (k n) -> p k n", k=8)
        DG = SC[:, 4096:4096 + 3968].rearrange("p (k n) -> p k n", k=31)
        convS = SC[:, 8192:16384].bitcast(F32).rearrange("p (c n) -> p c n", c=4)
        glu = SC[:, 16384:16384 + 4216].rearrange("p (c n) -> p c n", c=4)
        WS32 = SC[:, 0:2048].bitcast(F32).rearrange("p (h s) -> p h s", h=8)
        WSb = SC[:, 2048:3072].rearrange("p (h s) -> p h s", h=8)
        CW = SC[:, 8192:9216].bitcast(F32)
        ROWS = SC[:, 9216:9472].bitcast(F32)
        PTb = [P[6][:].bitcast(BF16), P[7][:].bitcast(BF16), P[0][:].bitcast(BF16), P[1][:].bitcast(BF16)]

        PE = Q(nc, nc.tensor, st, "s_pe")
        ACT = Q(nc, nc.scalar, st, "s_act")
        DVE = Q(nc, nc.vector, st, "s_dve")
        POOL = Q(nc, nc.gpsimd, st, "s_pool")
        SP = Q(nc, nc.sync, st, "s_sp")
        par = DmaSem(nc, st, "d_par")
        ring_sems = [DmaSem(nc, st, f"d_ring{i}") for i in range(R)]
        xl = [DmaSem(nc, st, f"d_xl{t}") for t in range(8)]
        xs = [DmaSem(nc, st, f"d_xs{t}") for t in range(8)]

        st_pos = [0]

        def stc(k):
            if st_pos[0] + k > 256:
                st_pos[0] = 0
            a = ST[:, st_pos[0]:st_pos[0] + k]
            st_pos[0] += k
            return a

        V = nc.vector
        A = nc.scalar
        T = nc.tensor
        G = nc.gpsimd
        S = nc.sync

        def chain(src, k, scale, eps, waits):
            v, xh, y0, y1, y2 = stc(k), stc(k), stc(k), stc(k), stc(k)
            DVE.wait(waits)
            t = DVE.done(V.tensor_scalar(out=v, in0=src, scalar1=scale, scalar2=eps, op0=ALU.mult, op1=ALU.add))
            V.tensor_scalar(out=xh, in0=src, scalar1=-0.5 * scale, scalar2=-0.5 * eps, op0=ALU.mult, op1=ALU.add)
            DVE.wait(t)
            t = DVE.done(V.tensor_scalar(out=y0.bitcast(I32), in0=v.bitcast(I32), scalar1=-0.5, scalar2=MAGIC,
                                         op0=ALU.mult, op1=ALU.add))
            cur = y0
            for nxt in (y1, y2):
                a = stc(k)
                DVE.wait(t)
                if k == 1:
                    t = DVE.done(V.scalar_tensor_tensor(out=a, in0=cur, scalar=xh, in1=cur, op0=ALU.mult, op1=ALU.mult))
                else:
                    b = stc(k)
                    t = DVE.done(V.tensor_tensor(out=b, in0=cur, in1=cur, op=ALU.mult))
                    DVE.wait(t)
                    t = DVE.done(V.tensor_tensor(out=a, in0=b, in1=xh, op=ALU.mult))
                DVE.wait(t)
                t = DVE.done(V.scalar_tensor_tensor(out=nxt, in0=a, scalar=1.5, in1=cur, op0=ALU.add, op1=ALU.mult))
                cur = nxt
            return cur, t

        def pchain(src, scale, eps, waits, mean=None):
            v, y = stc(1), stc(1)
            POOL.wait(waits)
            t = POOL.done(G.tensor_scalar(out=v, in0=src, scalar1=scale, scalar2=eps, op0=ALU.mult, op1=ALU.add))
            POOL.wait(t)
            t = POOL.done(G.tensor_tensor(out=y, in0=v, in1=NH[:, 0:1], op=ALU.pow))
            if mean is None:
                return y, t
            nmr, nm = stc(1), stc(1)
            tn = POOL.done(G.tensor_scalar(out=nm, in0=mean, scalar1=-1.0, scalar2=0.0, op0=ALU.mult, op1=ALU.add))
            POOL.wait(t, tn)
            t = POOL.done(G.tensor_tensor(out=nmr, in0=nm, in1=y, op=ALU.mult))
            return y, nmr, t

        plan = []

        def v3(off, a, b):
            return lambda s: s[:, off:off + a * b].rearrange("p (a b) -> p a b", a=a)

        def kp(ap):
            return ap.rearrange("(k p) n -> p k n", p=128)

        def plan_ada(sub):
            base = 3 * sub * D
            for c0 in (base, base + 512, base + 1024, base + 1536, base + 2048, base + 2560):
                plan.append([(v3(0, 8, 512), kp(w_ada[:, c0:c0 + 512]))])

        def plan_ffn(sub, inserts=None):
            wi, wo = w_in[sub], w_out[sub]
            for q in range(11):
                plan.append([(v3(0, 8, 256), kp(wi[:, 256 * q:256 * q + 256])),
                             (v3(2048, 8, 256), kp(wi[:, DFF + 256 * q:DFF + 256 * q + 256]))])
                if inserts and q in inserts:
                    plan.append(inserts[q])
            for r in range(6):
                nf = 4 if r < 5 else 2
                plan.append([(v3(0, nf, 1024), kp(wo[512 * r:512 * r + 128 * nf, :]))])

        def plan_mix():
            for c0 in (1024, 1536, 0, 512):
                plan.append([(v3(0, 8, 512), kp(w_mix_in[:, c0:c0 + 512]))])
            for r in range(2):
                plan.append([(v3(0, 4, 1024), kp(w_mix_out[512 * r:512 * r + 512, :]))])

        plan_ada(0)
        gate0 = [plan.pop(), plan.pop()][::-1]
        for blk in range(nblk):
            plan_ffn(0, inserts={1: gate0[0], 2: gate0[1]} if blk == 0 else None)
            if nsub >= 2:
                if blk == 0:
                    plan_ada(1)
                plan_mix()
            if nsub >= 3:
                if blk == 0:
                    plan_ada(2)
                plan_ffn(2)

        ring = {"issued": 0, "next": 0, "tok": {}, "slot": {}, "free": list(range(R)), "rel": [None] * R}

        def pump():
            while ring["issued"] < len(plan) and ring["free"]:
                n = ring["issued"]
                s_i = ring["free"].pop(0)
                POOL.wait(ring["rel"][s_i])
                tok = None
                for fn, src in plan[n]:
                    tok = ring_sems[s_i].add(G.dma_start(out=fn(slots[s_i]), in_=src))
                ring["tok"][n] = tok
                ring["slot"][n] = s_i
                ring["issued"] += 1

        def nextchunk():
            n = ring["next"]
            ring["next"] += 1
            assert n < ring["issued"], "chunk not issued yet (ring too small)"
            return n, slots[ring["slot"][n]], ring["tok"][n]

        def release(n, tok):
            s_i = ring["slot"][n]
            ring["rel"][s_i] = tok
            ring["free"].append(s_i)
            pump()

        def gp(inst):
            t_ = POOL.done(inst)
            POOL.wait(t_)
            return t_

        pump()
        gp(G.memset(NH[:], -0.5))
        gp(G.memset(identf[:], 0.0))
        gp(G.affine_select(out=identf[:], in_=identf[:], pattern=[[-1, 128]], compare_op=ALU.not_equal, fill=1.0,
                           base=0, channel_multiplier=1))
        gp(G.memset(SEL[:], 0.0))
        for s_ in range(2):
            for j in range(3):
                gp(G.affine_select(out=SEL[:, s_, :], in_=SEL[:, s_, :], pattern=[[0, 128]], compare_op=ALU.not_equal,
                                   fill=1.0, base=-(32 * j + s_), channel_multiplier=1))
        gp(G.memset(IND[:], 1.0))
        gp(G.affine_select(out=IND[:], in_=IND[:], pattern=[[1, 512]], compare_op=ALU.is_ge, fill=0.0, base=0,
                           channel_multiplier=-64))
        tok_gconst = gp(G.affine_select(out=IND[:], in_=IND[:], pattern=[[-1, 512]], compare_op=ALU.is_ge,
                                        fill=0.0, base=63, channel_multiplier=64))
        def r8(v_):
            return v_.rearrange("(a b) -> a b", b=128)

        rows_src = [g_pre[0], g_pre[1], g_pre[2],
                    b_ada[0:1024], b_ada[1024:2048], b_ada[3072:4096], b_ada[4096:5120], b_ada[6144:7168],
                    b_ada[7168:8192]]
        r0 = 0
        for v_ in rows_src:
            par.add(S.dma_start(out=ROWS[r0:r0 + 8, :], in_=r8(v_)))
            r0 += 8
        for v_ in (conv_b, g_oa, g_ob):
            par.add(S.dma_start(out=ROWS[r0:r0 + 4, :], in_=r8(v_)))
            r0 += 4
        for s_ in range(2):
            par.add(S.dma_start(out=ROWS[r0:r0 + 8, :], in_=r8(c_in[s_, :])))
            r0 += 8
        NROWS = r0
        C_GPRE, C_BADA, C_CONVB, C_GOA, C_GOB, C_C = 0, 24, 72, 76, 80, 84
        par.add(S.dma_start(out=CW[0:31, :], in_=conv_w))
        par.add(S.dma_start(out=WS32, in_=w_sp.rearrange("h t s -> t h s")))
        par.add(S.dma_start(out=bsp[:], in_=b_sp))
        for i, v_ in enumerate((gmlp_g, gmlp_b, cn_g, cn_b)):
            par.add(S.dma_start(out=GB[:, i, :], in_=v_.partition_broadcast(128)))
        for j in range(3):
            for s_ in range(2):
                p0 = 32 * j + s_
                par.add(S.dma_start(out=GBP[p0:p0 + 1, 0, :],
                                    in_=b_ada[(3 * j + 2) * D:(3 * j + 3) * D].rearrange("(a n) -> a n", a=1)))
                par.add(S.dma_start(out=GBP[p0:p0 + 1, 1, :], in_=g_post[j].rearrange("(a n) -> a n", a=1)))
        tok_par = par.tok()

        x_ready = [None] * 8
        for t in range(8):
            x_ready[t] = xl[t].add(S.dma_start(out=xres[:, t, :], in_=x_in[t * 128:(t + 1) * 128, :]))

        pump()

        DVE.wait(tok_gconst)
        tok_ident = DVE.done(V.tensor_copy(ident[:], identf[:]))
        POOL.wait(tok_par)
        tok_mask = POOL.done(G.affine_select(out=WS32, in_=WS32, pattern=[[0, 8], [-1, 128]], compare_op=ALU.is_ge,
                                             fill=0.0, base=0, channel_multiplier=1))
        DVE.wait(tok_mask)
        tok_wsb = DVE.done(V.tensor_copy(WSb, WS32))
        PE.wait(tok_wsb, tok_ident)
        for h in range(8):
            i_ = T.transpose(PTb[0][:, h * 128:(h + 1) * 128], WSb[:, h, :], ident[:])
        tok_pe = PE.done(i_)
        DVE.wait(tok_pe)
        tok_wst = DVE.done(V.tensor_copy(WsT[:].rearrange("p h t -> p (h t)"), PTb[0][:, 0:1024]))
        PE.wait(tok_par, tok_gconst)
        i_ = T.transpose(P[0][:, 0:NROWS], ROWS[0:NROWS, :], identf[0:NROWS, 0:NROWS])
        for c in range(4):
            i_ = T.transpose(P[1][:, c * 31:(c + 1) * 31], CW[0:31, c * 128:(c + 1) * 128], identf[0:31, 0:31])
        tok_pe = PE.done(i_)
        DVE.wait(tok_pe)
        V.tensor_copy(COLS[:, 0:NROWS], P[0][:, 0:NROWS])
        tok_cols = DVE.done(V.tensor_scalar(out=convw[:].rearrange("p c k -> p (c k)"), in0=P[1][:, 0:124],
                                            scalar1=0.5, scalar2=None, op0=ALU.mult))
        for j in (0, 2):
            DVE.wait(tok_par)
            tok_gbp = DVE.done(V.tensor_scalar(out=GBP[32 * j:32 * j + 2, 1, :], in0=GBP[32 * j:32 * j + 2, 1, :],
                                               scalar1=0.5, scalar2=None, op0=ALU.mult))
        ACT.wait(tok_cols)
        for s_ in range(2):
            i_ = A.activation(out=cTb[:, :, s_], in_=COLS[:, C_C + 8 * s_:C_C + 8 * s_ + 8], func=AF.Silu)
        tok_ct = ACT.done(i_)

        state = {"scratch_free": [tok_pe, tok_wst, tok_cols], "yset_free": [None, None], "gu_free": [tok_cols, None],
                 "pt_free": [tok_wst, None, None, None], "presq": None, "ssx_idx": 0, "mod_tok": [None, None, None], "gr_tok": [None, None, None],
                 "gg_free": None, "tmp_free": None, "sg_free": [None, None], "ht_free": None,
                 "pending_ada": 1, "pending_ada2": True}

        def ada_steps(sub, pa, pg, bank_wait):
            adav = ADA[:, sub * 16:(sub + 1) * 16, :]
            pav = pa[:, 0:32].rearrange("p (f s) -> p f s", s=2)
            p0 = 32 * sub
            loc = {"t1": None, "free": []}

            def feat(k):
                vi, hc = k // 2, k % 2
                n, slot, tk = nextchunk()
                w = slot[:, :].rearrange("p (k n) -> p k n", k=8)
                PE.wait(tk, tok_ct, bank_wait)
                for f4 in range(4):
                    fc = hc * 4 + f4
                    col = (vi * 8 + fc) * 2
                    for kc in range(8):
                        i_ = T.matmul(pa[:, col:col + 2], lhsT=w[:, kc, f4 * 128:(f4 + 1) * 128], rhs=cTb[:, kc, :],
                                      start=(kc == 0), stop=(kc == 7))
                last = PE.done(i_)
                release(n, last)
                if k < 3:
                    return
                DVE.wait(last, tok_cols)
                for vi_ in range(2):
                    bcol = C_BADA + (2 * sub + vi_) * 8
                    for s_ in range(2):
                        i_ = V.tensor_tensor(out=adav[:, vi_ * 8:(vi_ + 1) * 8, s_], in0=pav[:, vi_ * 8:(vi_ + 1) * 8, s_],
                                             in1=COLS[:, bcol:bcol + 8], op=ALU.add)
                t1 = DVE.done(i_)
                loc["free"].append(t1)
                DVE.wait(t1)
                for s_ in range(2):
                    V.tensor_copy(MOD[:, sub, 1, :, s_], adav[:, 0:8, s_])
                    i_ = V.scalar_tensor_tensor(out=MOD[:, sub, 0, :, s_], in0=adav[:, 8:16, s_], scalar=1.0,
                                                in1=COLS[:, C_GPRE + 8 * sub:C_GPRE + 8 * sub + 8], op0=ALU.add,
                                                op1=ALU.mult)
                state["mod_tok"][sub] = DVE.done(i_)

            def gate(hc):
                n, slot, tk = nextchunk()
                w = slot[:, :].rearrange("p (k n) -> p k n", k=8)
                PE.wait(tk, tok_ct, bank_wait, loc["t1"])
                for kc in range(8):
                    i_ = T.matmul(pg[p0:p0 + 2, :], lhsT=cTb[:, kc, :], rhs=w[:, kc, :], start=(kc == 0), stop=(kc == 7))
                tpe = PE.done(i_)
                release(n, tpe)
                DVE.wait(tpe, tok_par, tok_gbp)
                grv = GR[p0:p0 + 2, hc * 512:(hc + 1) * 512]
                t1 = DVE.done(V.tensor_tensor(out=grv, in0=pg[p0:p0 + 2, :], in1=GBP[p0:p0 + 2, 0, hc * 512:(hc + 1) * 512],
                                              op=ALU.add))
                loc["t1"] = t1
                DVE.wait(t1)
                t2 = DVE.done(V.tensor_tensor(out=grv, in0=grv, in1=GBP[p0:p0 + 2, 1, hc * 512:(hc + 1) * 512],
                                              op=ALU.mult))
                if hc == 1:
                    state["gr_tok"][sub] = t2
                    loc["free"].append(t2)

            steps = [lambda k=k: feat(k) for k in range(4)] + [lambda h=h: gate(h) for h in range(2)]
            return steps, loc

        def ada(sub):
            steps, loc = ada_steps(sub, P[0], P[1], [state["gu_free"][0]])
            for st_ in steps:
                st_()
            state["gu_free"][0] = loc["free"]

        def presquare(t):
            if state["presq"] is None:
                state["ssx_idx"] ^= 1
                state["presq"] = {}
            col = 8 * state["ssx_idx"] + t
            ACT.wait(x_ready[t], state["ht_free"])
            state["presq"][t] = ACT.done(A.activation(out=hT[:, t, :], in_=xres[:, t, :], func=AF.Square,
                                                      accum_out=SSX[:, col:col + 1]))

        def prenorm(sub, s_, defer=False):
            for t in range(4):
                if state["presq"] is None or t not in state["presq"]:
                    presquare(t)
            ssx = SSX[:, 8 * state["ssx_idx"]:8 * state["ssx_idx"] + 8]
            rs = [chain(ssx[:, 0:4], 4, 1.0 / D, EPS, [state["presq"][3]]), None]
            ht_ready = []
            xn_free = state["scratch_free"]
            tpe = None

            def trans(half, t_xn):
                evs = []
                tpe_ = None
                for kc in range(8):
                    pb = kc % 4
                    PE.wait(t_xn, state["pt_free"][pb], tok_ident, state["yset_free"][1], state["gu_free"][0])
                    for tt in range(4):
                        i_ = T.transpose(PTb[pb][:, tt * 128:(tt + 1) * 128], xn[:, tt, kc * 128:(kc + 1) * 128], ident[:])
                    tpe_ = PE.done(i_)
                    gm = MOD[:, sub, 0, kc, s_:s_ + 1]
                    sh = MOD[:, sub, 1, kc, s_:s_ + 1]
                    dst = hT[:, kc, half * 512:(half + 1) * 512]
                    if pb % 2 == 0:
                        ACT.wait(tpe_, state["mod_tok"][sub])
                        te = ACT.done(A.activation(out=dst, in_=PTb[pb][:, 0:512], func=AF.Identity, scale=gm, bias=sh))
                    else:
                        DVE.wait(tpe_, state["mod_tok"][sub])
                        te = DVE.done(V.tensor_scalar(out=dst, in0=PTb[pb][:, 0:512], scalar1=gm, scalar2=sh,
                                                      op0=ALU.mult, op1=ALU.add))
                    state["pt_free"][pb] = te
                    evs.append(te)
                state["gu_free"][0] = [state["gu_free"][0], state["pt_free"][2], state["pt_free"][3]]
                return evs[-4:], tpe_

            for half in range(2):
                rstd4, tR = rs[half]
                ACT.wait(tR, xn_free)
                DVE.wait(tR, xn_free)
                ia = iv = None
                for tt in range(4):
                    t = half * 4 + tt
                    if tt % 2 == 0:
                        ia = A.activation(out=xn[:, tt, :], in_=xres[:, t, :], func=AF.Identity, scale=rstd4[:, tt:tt + 1])
                    else:
                        iv = V.tensor_scalar(out=xn[:, tt, :], in0=xres[:, t, :], scalar1=rstd4[:, tt:tt + 1], scalar2=None,
                                             op0=ALU.mult)
                t_xn = [ACT.done(ia), DVE.done(iv)]
                if half == 0:
                    for t in range(4, 8):
                        if t not in state["presq"]:
                            presquare(t)
                    rs[1] = chain(ssx[:, 4:8], 4, 1.0 / D, EPS, [state["presq"][7]])
                    state["presq"] = None
                if half == 1 and defer:
                    def fin(t_xn=t_xn):
                        evs4, tpe2 = trans(1, t_xn)
                        ht_ready.append(evs4)
                        return tpe2
                    return ht_ready, tpe, fin
                evs4, tpe = trans(half, t_xn)
                xn_free = [tpe]
                ht_ready.append(evs4)
            return ht_ready, tpe, None

        def make_gg(sub, s_, banks=None, waits=None):
            p0 = 32 * sub
            tks = []
            for half in range(2):
                bank = P[4 + half] if banks is None else banks[half]
                PE.wait(state["gr_tok"][sub], state["yset_free"][0], tok_gconst, waits)
                tpe = PE.done(T.matmul(bank[:, :], lhsT=SEL[p0:p0 + 2, s_, :], rhs=GR[p0:p0 + 2, half * 512:(half + 1) * 512],
                                       start=True, stop=True))
                ACT.wait(tpe, state["gg_free"])
                tks.append(ACT.done(A.activation(out=gg[:, half * 512:(half + 1) * 512], in_=bank[:, :], func=AF.Identity)))
            if banks is None:
                state["yset_free"][0] = [state["yset_free"][0], tks[-1]]
            return tks[-1]

        def epilogue(t, yset, tpe, tok_gg, banks=None):
            bA, bB = (P[4 + 2 * yset], P[5 + 2 * yset]) if banks is None else banks
            ss2 = stc(2)
            ACT.wait(tpe, state["ht_free"])
            A.activation(out=hT[:, 0, 0:512], in_=bA[:, :], func=AF.Square, accum_out=ss2[:, 0:1])
            tA = ACT.done(A.activation(out=hT[:, 0, 512:1024], in_=bB[:, :], func=AF.Square, accum_out=ss2[:, 1:2]))
            ss = stc(1)
            POOL.wait(tA)
            t1 = POOL.done(G.tensor_tensor(out=ss, in0=ss2[:, 0:1], in1=ss2[:, 1:2], op=ALU.add))
            rstd, tR = pchain(ss, 1.0 / D, EPS, [t1])
            tmp = T32[:, 2:4, :]
            DVE.wait(tpe, tok_gg, state["tmp_free"])
            DVE.wait(tA)
            V.tensor_tensor(out=tmp[:, 0, :], in0=gg[:, 0:512], in1=bA[:, :], op=ALU.mult)
            t_tmp = DVE.done(V.tensor_tensor(out=tmp[:, 1, :], in0=gg[:, 512:1024], in1=bB[:, :], op=ALU.mult))
            if banks is None:
                state["yset_free"][yset] = [tA, t_tmp]
            DVE.wait(tR, t_tmp)
            tX = DVE.done(V.scalar_tensor_tensor(out=xres[:, t, :], in0=tmp.rearrange("p a n -> p (a n)"), scalar=rstd,
                                                 in1=xres[:, t, :], op0=ALU.mult, op1=ALU.add))
            state["tmp_free"] = tX
            state["gg_free"] = tX
            x_ready[t] = tX
            return [tA, t_tmp]

        def ffn(sub, s_):
            last_sub = (sub + 1 == nsub) or (sub == 2)
            ht_ready, _, fin = prenorm(sub, s_, defer=True)
            it = 0
            for q in range(11):
                n, slot, tk = nextchunk()
                wg = slot[:, 0:2048].rearrange("p (k n) -> p k n", k=8)
                wu = slot[:, 2048:4096].rearrange("p (k n) -> p k n", k=8)
                tpe = None
                order = [(sf, half) for half in range(2) for sf in range(2)] if q == 0 else \
                        [(sf, half) for sf in range(2) for half in range(2)]
                for gi, (sf, half) in enumerate(order):
                    fc = 2 * q + sf
                    if q == 0 and gi == 2:
                        fin()
                    if True:
                        ps = it % 2
                        bg, bu = P[2 * ps], P[2 * ps + 1]
                        PE.wait(tk, ht_ready[half], state["gu_free"][ps])
                        rhs_cols = slice(half * 512, (half + 1) * 512)
                        for kc in range(8):
                            T.matmul(bg[:, :], lhsT=wg[:, kc, sf * 128:(sf + 1) * 128], rhs=hT[:, kc, rhs_cols],
                                     start=(kc == 0), stop=(kc == 7))
                        for kc in range(8):
                            i_ = T.matmul(bu[:, :], lhsT=wu[:, kc, sf * 128:(sf + 1) * 128], rhs=hT[:, kc, rhs_cols],
                                          start=(kc == 0), stop=(kc == 7))
                        tpe = PE.done(i_)
                        sg = T32[:, ps, :]
                        ACT.wait(tpe, state["sg_free"][ps])
                        tA = ACT.done(A.activation(out=sg, in_=bg[:, :], func=AF.Silu))
                        DVE.wait(tA, tpe)
                        tD = DVE.done(V.tensor_tensor(out=actT[:, fc, rhs_cols], in0=sg, in1=bu[:, :], op=ALU.mult))
                        state["gu_free"][ps] = tD
                        state["sg_free"][ps] = tD
                        it += 1
                release(n, tpe)
                if sub == 0 and state["gate0_steps"] is not None and q in (1, 2):
                    state["gate0_steps"][0][3 + q]()
                    if q == 2:
                        state["yset_free"][0] = [state["yset_free"][0], state["gate0_steps"][1]["free"]]
                        state["gate0_steps"] = None
            act_ready = tD
            state["ht_free"] = tpe
            w2 = []
            for r in range(6):
                n, slot, tk = nextchunk()
                w2.append((n, slot[:, :].rearrange("p (f n) -> p f n", f=4), tk))
            tok_gg = make_gg(sub, s_)
            tpe = None
            side = None
            if state["pending_ada"] is not None and sub == 0 and nsub >= 2:
                side = ada_steps(state["pending_ada"], P[0], P[1], [state["gu_free"][0], state["gu_free"][1]])
                state["pending_ada"] = None
            for t in range(8):
                ys = t % 2
                bA, bB = P[4 + 2 * ys], P[5 + 2 * ys]
                PE.wait(state["yset_free"][ys], act_ready, [w[2] for w in w2], state["pt_free"])
                for fc in range(22):
                    wv = w2[fc // 4][1]
                    lhs = actT[:, fc, t * 128:(t + 1) * 128]
                    T.matmul(bA[:, :], lhsT=lhs, rhs=wv[:, fc % 4, 0:512], start=(fc == 0), stop=(fc == 21))
                    i_ = T.matmul(bB[:, :], lhsT=lhs, rhs=wv[:, fc % 4, 512:1024], start=(fc == 0), stop=(fc == 21))
                tpe = PE.done(i_)
                epilogue(t, ys, tpe, tok_gg)
                if side is not None and t < 6:
                    side[0][t]()
                if not last_sub:
                    if t >= 1:
                        presquare(t - 1)
            for (n, _, _) in w2:
                release(n, tpe)
            if side is not None:
                state["gu_free"][0] = [state["gu_free"][0], side[1]["free"]]
            state["scratch_free"] = [tpe]

        def tail_front(src, kind, par_, waits):
            ss = stc(1)
            ACT.wait(waits, state["ynb_free"][kind][par_])
            tA = ACT.done(A.activation(out=ynb[:, kind, par_, :], in_=src, func=AF.Square, accum_out=ss))
            rstd, tR = pchain(ss, 1.0 / 512, EPS, [tA])
            ACT.wait(tR)
            return ACT.done(A.activation(out=ynb[:, kind, par_, :], in_=src, func=AF.Identity, scale=rstd))

        def tail_back(tY, kind, par_, t, cbase, gcol):
            PE.wait(tY, state["pt_free"][kind], state["yset_free"][1])
            for c in range(4):
                i_ = T.transpose(PTb[kind][:, c * 128:(c + 1) * 128], ynb[:, kind, par_, c * 128:(c + 1) * 128], ident[:])
            tpe = PE.done(i_)
            state["ynb_free"][kind][par_] = tpe
            te = None
            for c in range(4):
                dst = yT[:, cbase + c, t * 128:(t + 1) * 128]
                srcp = PTb[kind][:, c * 128:(c + 1) * 128]
                gcl = COLS[:, gcol + c:gcol + c + 1]
                if kind == 0:
                    ACT.wait(tpe)
                    te = ACT.done(A.activation(out=dst, in_=srcp, func=AF.Identity, scale=gcl))
                else:
                    DVE.wait(tpe)
                    te = DVE.done(V.tensor_scalar(out=dst, in0=srcp, scalar1=gcl, scalar2=None, op0=ALU.mult))
            state["pt_free"][kind] = te
            return te

        def lnorm_a(psrc, dst, gi, waits, dst_free=None):
            st6 = stc(6)
            mv = stc(2)
            DVE.wait(waits)
            t1 = DVE.done(V.bn_stats(st6, psrc))
            DVE.wait(t1)
            t2 = DVE.done(V.bn_aggr(mv, st6))
            rstd, tR = pchain(mv[:, 1:2], 1.0, EPS, [t2])
            DVE.wait(t2, dst_free)
            tA_ = DVE.done(V.scalar_tensor_tensor(out=dst, in0=psrc, scalar=mv[:, 0:1], in1=GB[:, gi, :],
                                                  op0=ALU.subtract, op1=ALU.mult))
            return rstd, tR, tA_

        def mixer(blk, s_):
            sub = 1
            ht_ready, t_tr, _ = prenorm(sub, s_)
            state.setdefault("ynb_free", [[None, None], [None, None]])
            DVE.wait(t_tr, state["scratch_free"])
            if blk % 2 == 0:
                t_halo = DVE.done(V.memset(glu[:, :, 0:30], 0.0))
            else:
                t_halo = DVE.done(V.tensor_copy(glu[:, :, 0:30], HALO[:]))
            dg_free = [t_tr, t_tr]
            taps = [range(0, 16), range(16, 31)]

            def dg_gen(u):
                c_, part_ = u // 2, u % 2
                DVE.wait(dg_free[part_], tok_cols)
                for k in taps[part_]:
                    i_ = V.tensor_scalar(out=DG[:, k, :], in0=ident[:], scalar1=convw[:, c_, k:k + 1], scalar2=None,
                                         op0=ALU.mult)
                return DVE.done(i_)

            t_dgs = {0: dg_gen(0)}
            nA, slotA, tkA = nextchunk()
            nG, slotG, tkG = nextchunk()
            wA = slotA[:, :].rearrange("p (k n) -> p k n", k=8)
            wG = slotG[:, :].rearrange("p (k n) -> p k n", k=8)
            it = 0
            for c in range(4):
                for half in range(2):
                    ps = it % 2
                    ba, bg = P[2 * ps], P[2 * ps + 1]
                    cols = slice(half * 512, (half + 1) * 512)
                    PE.wait(tkA, tkG, ht_ready[half], state["gu_free"][ps])
                    for kc in range(8):
                        T.matmul(ba[:, :], lhsT=wA[:, kc, c * 128:(c + 1) * 128], rhs=hT[:, kc, cols], start=(kc == 0),
                                 stop=(kc == 7))
                    for kc in range(8):
                        i_ = T.matmul(bg[:, :], lhsT=wG[:, kc, c * 128:(c + 1) * 128], rhs=hT[:, kc, cols],
                                      start=(kc == 0), stop=(kc == 7))
                    tpe = PE.done(i_)
                    tg = T32[:, ps, :]
                    ACT.wait(tpe, state["sg_free"][ps])
                    tA = ACT.done(A.activation(out=tg, in_=bg[:, :], func=AF.Tanh, scale=0.5))
                    DVE.wait(tA, tpe, t_halo)
                    tD = DVE.done(V.scalar_tensor_tensor(out=glu[:, c, 30 + half * 512:30 + (half + 1) * 512], in0=tg,
                                                         scalar=1.0, in1=ba[:, :], op0=ALU.add, op1=ALU.mult))
                    state["gu_free"][ps] = tD
                    state["sg_free"][ps] = tD
                    it += 1
            release(nA, tpe)
            release(nG, tpe)
            DVE.wait(tD)
            t_glu = DVE.done(V.tensor_copy(HALO[:], glu[:, :, 1024:1054]))
            nU, slotU, tkU = nextchunk()
            nV, slotV, tkV = nextchunk()
            wU = slotU[:, :].rearrange("p (k n) -> p k n", k=8)
            wV = slotV[:, :].rearrange("p (k n) -> p k n", k=8)
            uv_free = [state["gu_free"][0], state["gu_free"][1]]
            us_free = [state["sg_free"][0], state["sg_free"][1]]
            vn_free = [None, None]
            sh_ = {"z_free": [state["yset_free"][1], state["pt_free"][1]], "cvt_free": None, "t_uv": None, "t_cv": None}
            tk = {}
            uS = [T32[:, 0, :], T32[:, 1, :]]
            tbA = [T32[:, 2, :], T32[:, 3, :]]
            tbB = [T32[:, 3, :], T32[:, 4, :], T32[:, 5, :]]

            def A1(t):
                p_ = t % 2
                bu, bv = P[2 * p_], P[2 * p_ + 1]
                tcols = slice(t * 128, (t + 1) * 128)
                PE.wait(tkU, tkV, ht_ready[0], ht_ready[1], uv_free[p_])
                for kc in range(8):
                    T.matmul(bu[:, :], lhsT=hT[:, kc, tcols], rhs=wU[:, kc, :], start=(kc == 0), stop=(kc == 7))
                for kc in range(8):
                    i_ = T.matmul(bv[:, :], lhsT=hT[:, kc, tcols], rhs=wV[:, kc, :], start=(kc == 0), stop=(kc == 7))
                t_uv = PE.done(i_)
                sh_["t_uv"] = t_uv
                ACT.wait(t_uv, us_free[p_])
                tk["u", t] = ACT.done(A.activation(out=uS[p_], in_=bu[:, :], func=AF.Identity))
                rstd, tR, tA_ = lnorm_a(bv[:, :], tbA[p_], 0, [t_uv], dst_free=tk.get(("yA", t - 2)))
                uv_free[p_] = [tk["u", t], tA_]
                DVE.wait(vn_free[p_], tR, tA_)
                tk["vn", t] = DVE.done(V.scalar_tensor_tensor(out=vn[:, p_, :], in0=tbA[p_], scalar=rstd, in1=GB[:, 1, :],
                                                              op0=ALU.mult, op1=ALU.add))

            def A2(t):
                p_ = t % 2
                PE.wait(tk["vn", t], sh_["z_free"], tok_wst, tok_par, tok_gconst)
                T.matmul(P[7][:, :], lhsT=bsp[0:8, :], rhs=IND[0:8, :], start=True, stop=False)
                for h in range(8):
                    i_ = T.matmul(P[7][:, h * 64:(h + 1) * 64], lhsT=WsT[:, h, :], rhs=vn[:, p_, h * 64:(h + 1) * 64],
                                  start=False, stop=(h == 7))
                t_z = PE.done(i_)
                vn_free[p_] = t_z
                DVE.wait(t_z, tk["u", t])
                t_ya = DVE.done(V.tensor_tensor(out=tbA[p_], in0=uS[p_], in1=P[7][:, :], op=ALU.mult))
                sh_["z_free"] = t_ya
                us_free[p_] = t_ya
                tk["yA", t] = tail_front(tbA[p_], 0, p_, [t_ya])

            cv_free = [state["yset_free"][0], state["yset_free"][0]]
            t_cv = None
            evA = evB = None
            for unit in range(8):
                c, part = unit // 2, unit % 2
                if unit + 1 < 8:
                    t_dgs[unit + 1] = dg_gen(unit + 1)
                t_dg = t_dgs[unit]
                for half in range(2):
                    bank = P[4 + half]
                    PE.wait(t_dg, t_glu, cv_free[half])
                    for k in taps[part]:
                        i_ = T.matmul(bank[:, :], lhsT=DG[:, k, :], rhs=glu[:, c, half * 512 + k:half * 512 + k + 512],
                                      start=(k == 0), stop=(k == 30))
                    tpe = PE.done(i_)
                    if part == 1:
                        ACT.wait(tpe)
                        t_cv = ACT.done(A.activation(out=convS[:, c, half * 512:(half + 1) * 512], in_=bank[:, :],
                                                     func=AF.Identity, bias=COLS[:, C_CONVB + c:C_CONVB + c + 1]))
                        cv_free[half] = t_cv
                dg_free[part] = tpe
                A1(unit)
                if unit >= 1:
                    A2(unit - 1)
                if unit >= 2:
                    evA = tail_back(tk["yA", unit - 2], 0, unit % 2, unit - 2, 0, C_GOA)
            A2(7)
            evA = tail_back(tk["yA", 6], 0, 0, 6, 0, C_GOA)
            evA = tail_back(tk["yA", 7], 0, 1, 7, 0, C_GOA)
            sh_["t_cv"] = t_cv
            sh_["cvt_free"] = [cv_free[0], cv_free[1]]
            state["pt_free"][1] = [state["pt_free"][1], sh_["z_free"]]
            def B_s0(t):
                p_ = t % 2
                tcols = slice(t * 128, (t + 1) * 128)
                PE.wait(sh_["t_cv"], sh_["cvt_free"])
                for c in range(4):
                    i_ = T.transpose(P[5][:, c * 128:(c + 1) * 128], convS[:, c, tcols], identf[:])
                t_cvt = PE.done(i_)
                rstd, tR, tA_ = lnorm_a(P[5][:, :], tbB[t % 3], 2, [t_cvt], dst_free=tk.get(("yB", t - 3)))
                sh_["cvt_free"] = tA_
                tk["lnB", t] = (rstd, tR, tA_)

            def B_s1(t):
                p_ = t % 2
                rstd, tR, tA_ = tk["lnB", t]
                DVE.wait(tR, tA_)
                tb_ = tbB[t % 3]
                t5 = DVE.done(V.scalar_tensor_tensor(out=tb_, in0=tb_, scalar=rstd, in1=GB[:, 3, :],
                                                     op0=ALU.mult, op1=ALU.add))
                ACT.wait(t5)
                t_yb = ACT.done(A.activation(out=tb_, in_=tb_, func=AF.Silu))
                ss = stc(1)
                ACT.wait(t_yb, state["ynb_free"][1][p_])
                tA = ACT.done(A.activation(out=ynb[:, 1, p_, :], in_=tb_, func=AF.Square, accum_out=ss))
                tk["sqB", t] = (ss, tA)

            def B_s2(t):
                p_ = t % 2
                ss, tA = tk["sqB", t]
                rstd, tR = pchain(ss, 1.0 / 512, EPS, [tA])
                ACT.wait(tR)
                tk["yB", t] = ACT.done(A.activation(out=ynb[:, 1, p_, :], in_=tbB[t % 3], func=AF.Identity, scale=rstd))

            release(nU, sh_["t_uv"])
            release(nV, sh_["t_uv"])
            nO0, slotO0, tkO0 = nextchunk()
            nO1, slotO1, tkO1 = nextchunk()
            wO = [slotO0[:, :].rearrange("p (k n) -> p k n", k=4), slotO1[:, :].rearrange("p (k n) -> p k n", k=4)]
            tok_gg = make_gg(sub, s_, banks=[P[4], P[6]], waits=[evA, sh_["cvt_free"], state["pt_free"][0]])
            state["tmp_free"] = [evA]
            yfree = [uv_free[0], uv_free[1]]
            evBs = {}
            o_tpe = {}
            last_pe = [None]

            def M4_mm(t):
                ys = t % 2
                bA, bB = P[2 * ys], P[2 * ys + 1]
                PE.wait(yfree[ys], evA, evBs[t], tkO0, tkO1)
                for kc in range(8):
                    lhs = yT[:, kc, t * 128:(t + 1) * 128]
                    wv = wO[kc // 4]
                    T.matmul(bA[:, :], lhsT=lhs, rhs=wv[:, kc % 4, 0:512], start=(kc == 0), stop=(kc == 7))
                    i_ = T.matmul(bB[:, :], lhsT=lhs, rhs=wv[:, kc % 4, 512:1024], start=(kc == 0), stop=(kc == 7))
                o_tpe[t] = PE.done(i_)
                last_pe[0] = o_tpe[t]

            TM = [T32[:, 0:2, :], T32[:, 0:2, :]]
            epi = {}

            def epi_a(t):
                ys = t % 2
                bA, bB = P[2 * ys], P[2 * ys + 1]
                tmp = TM[ys]
                DVE.wait(o_tpe[t], tok_gg, state["tmp_free"], epi.get(t - 1))
                V.tensor_tensor(out=tmp[:, 0, :], in0=gg[:, 0:512], in1=bA[:, :], op=ALU.mult)
                t_tmp = DVE.done(V.tensor_tensor(out=tmp[:, 1, :], in0=gg[:, 512:1024], in1=bB[:, :], op=ALU.mult))
                ss2 = stc(2)
                ACT.wait(t_tmp, state["ht_free"])
                A.activation(out=hT[:, 0, 0:512], in_=bA[:, :], func=AF.Square, accum_out=ss2[:, 0:1])
                tA = ACT.done(A.activation(out=hT[:, 0, 512:1024], in_=bB[:, :], func=AF.Square, accum_out=ss2[:, 1:2]))
                yfree[ys] = [tA, t_tmp]
                ss = stc(1)
                POOL.wait(tA)
                t1 = POOL.done(G.tensor_tensor(out=ss, in0=ss2[:, 0:1], in1=ss2[:, 1:2], op=ALU.add))
                rstd, tR = pchain(ss, 1.0 / D, EPS, [t1])
                tk["epi", t] = (rstd, tR, t_tmp)

            def epi_b(t):
                ys = t % 2
                rstd, tR, t_tmp = tk["epi", t]
                DVE.wait(tR, t_tmp)
                tX = DVE.done(V.scalar_tensor_tensor(out=xres[:, t, :], in0=TM[ys].rearrange("p a n -> p (a n)"),
                                                     scalar=rstd, in1=xres[:, t, :], op0=ALU.mult, op1=ALU.add))
                epi[t] = tX
                state["gg_free"] = tX
                x_ready[t] = tX

            side = None
            if state["pending_ada2"] and nsub >= 3:
                side = ada_steps(2, P[4], P[6], [tok_gg])
                state["pending_ada2"] = False
            for i in range(8 + 6):
                if side is not None and 2 <= i < 8:
                    side[0][i - 2]()
                if 0 <= i - 2 < 8:
                    B_s2(i - 2)
                if i < 8:
                    B_s0(i)
                if 0 <= i - 3 < 8:
                    t = i - 3
                    evBs[t] = tail_back(tk["yB", t], 1, t % 2, t, 4, C_GOB)
                if 0 <= i - 4 < 8:
                    M4_mm(i - 4)
                if 0 <= i - 1 < 8:
                    B_s1(i - 1)
                if 0 <= i - 5 < 8:
                    epi_b(i - 5)
                if 0 <= i - 4 < 8:
                    epi_a(i - 4)
                if nsub >= 3 and 0 <= i - 6 < 8:
                    presquare(i - 6)
            state["tmp_free"] = [epi[6], epi[7]]
            state["sg_free"] = [epi[6], epi[7]]
            us_free = [epi[6], epi[7]]
            tpe = last_pe[0]
            release(nO0, tpe)
            release(nO1, tpe)
            state["ht_free"] = sh_["t_uv"]
            state["gu_free"] = [yfree[0], yfree[1]]
            state["sg_free"] = [us_free[0], us_free[1]]
            side_free = side[1]["free"] if side is not None else None
            state["yset_free"][0] = [sh_["cvt_free"], tok_gg, side_free]
            state["yset_free"][1] = [sh_["z_free"], tok_gg, side_free]
            state["scratch_free"] = [tpe]


        steps0, loc0 = ada_steps(0, P[0], P[4], [state["gu_free"][0]])
        for k_ in range(4):
            steps0[k_]()
        state["gu_free"][0] = list(loc0["free"])
        state["gate0_steps"] = (steps0, loc0)
        for blk in range(nblk):
            s_ = blk // 2
            ffn(0, s_)
            if nsub >= 2:
                mixer(blk, s_)
            if nsub >= 3:
                ffn(2, s_)
            for t in range(8):
                SP.wait(x_ready[t])
                row0 = blk * TB + t * 128
                tk = xs[t].add(S.dma_start(out=out[row0:row0 + 128, :], in_=xres[:, t, :]))
                if blk + 1 < nblk:
                    SP.wait(tk)
                    row1 = (blk + 1) * TB + t * 128
                    x_ready[t] = xl[t].add(S.dma_start(out=xres[:, t, :], in_=x_in[row1:row1 + 128, :]))
        for t in range(8):
            SP.wait(xs[t].tok())
        assert ring["next"] == len(plan) == ring["issued"], (ring["next"], len(plan), ring["issued"])
    return nc


_W_NAMES = ["w_ada", "b_ada", "g_pre_f1", "g_post_f1", "w_f1_in", "w_f1_out", "g_pre_m", "g_post_m", "w_mix_in",
            "gmlp_norm_g", "gmlp_norm_b", "w_spatial", "b_spatial", "conv_w", "conv_b", "conv_norm_g", "conv_norm_b",
            "g_out_a", "g_out_b", "w_mix_out", "g_pre_f2", "g_post_f2", "w_f2_in", "w_f2_out"]


def make_in_maps(inputs, ncores=NCORES):
    x = np.ascontiguousarray(np.asarray(inputs["x"], dtype=np.float32))
    c = np.ascontiguousarray(np.asarray(inputs["c"], dtype=np.float32))
    shared = {k: np.ascontiguousarray(np.asarray(inputs[k], dtype=np.float32)[0]) for k in _W_NAMES}
    maps = []
    for i in range(ncores):
        m = dict(shared)
        m["x"] = x[2 * i:2 * i + 2].reshape(TOK, D)
        m["c"] = c[2 * i:2 * i + 2]
        maps.append(m)
    return maps


def kernel(**inputs):
    nc = build()
    maps = make_in_maps(inputs)
    res = run_bass_kernel_spmd(nc, maps, core_ids=list(range(NCORES)))
    outs = [np.asarray(r["out"], dtype=np.float32).reshape(2, 2048, D) for r in res.results]
    return np.concatenate(outs, axis=0)
```
